# Optimizing a Trainium2 kernel written in Bass

```python
import math
import jax, jax.numpy as jnp
from jax import lax
import numpy as np

D_MODEL = 1024
BATCH = 2
SEQ = 8192
DEPTH = 2

CTX_LEN = 256
GRID_W = 64

MLA_HEADS = 8
QK_NOPE_DIM = 64
QK_ROPE_DIM = 32
QK_HEAD_DIM = QK_NOPE_DIM + QK_ROPE_DIM
V_HEAD_DIM = 64
Q_LORA_RANK = 256
KV_LORA_RANK = 128
MLA_OUT = MLA_HEADS * V_HEAD_DIM

CONV_DIM = 512
CONV_K = 3

IN_SPLITS = [Q_LORA_RANK,
             Q_LORA_RANK + KV_LORA_RANK,
             Q_LORA_RANK + KV_LORA_RANK + QK_ROPE_DIM,
             Q_LORA_RANK + KV_LORA_RANK + QK_ROPE_DIM + CONV_DIM,
             Q_LORA_RANK + KV_LORA_RANK + QK_ROPE_DIM + 2 * CONV_DIM]
D_IN = Q_LORA_RANK + KV_LORA_RANK + QK_ROPE_DIM + 3 * CONV_DIM
KV_COL_START = Q_LORA_RANK
KV_COL_END = Q_LORA_RANK + KV_LORA_RANK + QK_ROPE_DIM
MIX_WIDTH = MLA_OUT + CONV_DIM

D_FF = 2816
N_MOD = 9
ROPE_BASE = 10000.0
EPS = 1e-6
Q_BLOCK = 128
ATTN_SCALE = 1.0 / math.sqrt(QK_HEAD_DIM)

kernel_name = "hybrid_mla_shortconv_macaron_dit"


def rmsnorm(x, w):
    x32 = x.astype(jnp.float32)
    r = lax.rsqrt(jnp.mean(x32 * x32, axis=-1, keepdims=True) + EPS)
    return (x32 * r).astype(x.dtype) * w


def modulate(h, shift, scale):
    return h * (1 + scale) + shift


def swiglu(h, w_i, w_o):
    g, u = jnp.split(h @ w_i, 2, axis=-1)
    return (jax.nn.silu(g) * u) @ w_o


def axial_rope_tables(rows, dtype):
    t = jnp.arange(rows * GRID_W)
    row = (t // GRID_W).astype(jnp.float32)
    col = (t % GRID_W).astype(jnp.float32)
    d_axis = QK_ROPE_DIM // 2
    inv = ROPE_BASE ** (-jnp.arange(0, d_axis, 2, dtype=jnp.float32) / d_axis)
    ar = row[:, None] * inv
    ac = col[:, None] * inv
    return tuple(a.astype(dtype) for a in (jnp.cos(ar), jnp.sin(ar), jnp.cos(ac), jnp.sin(ac)))


def rotate(x, cos, sin):
    x1, x2 = jnp.split(x, 2, axis=-1)
    return jnp.concatenate([x1 * cos - x2 * sin, x2 * cos + x1 * sin], axis=-1)


def axial_rope(x, tabs):
    cr, sr, cc, sc = tabs
    xr, xc = jnp.split(x, 2, axis=-1)
    return jnp.concatenate([rotate(xr, cr, sr), rotate(xc, cc, sc)], axis=-1)


def attend(q, k, v):
    s = jnp.einsum('bqhd,bkhd->bhqk', q, k, preferred_element_type=jnp.float32) * ATTN_SCALE
    p = jax.nn.softmax(s, axis=-1).astype(v.dtype)
    return jnp.einsum('bhqk,bkhd->bqhd', p, v)


def blocked_attend(q, k, v):
    b, l, h, d = q.shape
    nb = l // Q_BLOCK
    qb = q.reshape(b, nb, Q_BLOCK, h, d).transpose(1, 0, 2, 3, 4)
    ob = lax.map(lambda qi: attend(qi, k, v), qb)
    return ob.transpose(1, 0, 2, 3, 4).reshape(b, l, h, v.shape[-1])


def mla_queries(cq, q_norm_w, w_uq, tabs):
    b, l, _ = cq.shape
    q = (rmsnorm(cq, q_norm_w) @ w_uq).reshape(b, l, MLA_HEADS, QK_HEAD_DIM)
    q_nope, q_pe = q[..., :QK_NOPE_DIM], q[..., QK_NOPE_DIM:]
    if tabs is not None:
        q_pe = axial_rope(q_pe, tuple(t[:, None, :] for t in tabs))
    return jnp.concatenate([q_nope, q_pe], axis=-1)


def mla_keys_values(ckv, k_pe, kv_norm_w, w_ukv, tabs):
    b, l, _ = ckv.shape
    kv = (rmsnorm(ckv, kv_norm_w) @ w_ukv).reshape(b, l, MLA_HEADS, QK_NOPE_DIM + V_HEAD_DIM)
    k_nope, v = kv[..., :QK_NOPE_DIM], kv[..., QK_NOPE_DIM:]
    if tabs is not None:
        k_pe = axial_rope(k_pe, tabs)
    k_pe = jnp.broadcast_to(k_pe[:, :, None, :], (b, l, MLA_HEADS, QK_ROPE_DIM))
    return jnp.concatenate([k_nope, k_pe], axis=-1), v


def short_conv(u, w):
    up = jnp.pad(u, ((0, 0), (1, 1), (0, 0)))
    return up[:, :-2] * w[0] + up[:, 1:-1] * w[1] + up[:, 2:] * w[2]


def token_mixer(hl, hc, w_in, q_norm_w, kv_norm_w, w_uq, w_ukv, conv_w, w_out, tabs, ctx_out):
    b, l, _ = hl.shape
    cq_l, ckv_l, kpe_l, gb_l, gc_l, xv_l = jnp.split(hl @ w_in, IN_SPLITS, axis=-1)
    if ctx_out:
        cq_c, ckv_c, kpe_c, gb_c, gc_c, xv_c = jnp.split(hc @ w_in, IN_SPLITS, axis=-1)
    else:
        ckv_c, kpe_c = jnp.split(hc @ w_in[:, KV_COL_START:KV_COL_END], [KV_LORA_RANK], axis=-1)
    k_c, v_c = mla_keys_values(ckv_c, kpe_c, kv_norm_w, w_ukv, None)
    k_l, v_l = mla_keys_values(ckv_l, kpe_l, kv_norm_w, w_ukv, tabs)
    q_l = mla_queries(cq_l, q_norm_w, w_uq, tabs)
    k_all = jnp.concatenate([k_c, k_l], axis=1)
    v_all = jnp.concatenate([v_c, v_l], axis=1)
    att_l = blocked_attend(q_l, k_all, v_all).reshape(b, l, MLA_OUT)
    conv_l = gb_l * short_conv(gc_l * xv_l, conv_w)
    out_l = jnp.concatenate([att_l, conv_l], axis=-1) @ w_out
    if not ctx_out:
        return out_l, None
    q_c = mla_queries(cq_c, q_norm_w, w_uq, None)
    att_c = attend(q_c, k_c, v_c).reshape(hc.shape[0], hc.shape[1], MLA_OUT)
    conv_c = gb_c * short_conv(gc_c * xv_c, conv_w)
    out_c = jnp.concatenate([att_c, conv_c], axis=-1) @ w_out
    return out_l, out_c


def trunk_layer(xl, xc, ml, mc, norm_w, w_ffn1_in, w_ffn1_out, w_ffn2_in, w_ffn2_out,
                w_in, q_norm_w, kv_norm_w, w_uq, w_ukv, conv_w, w_out, tabs, last):
    sh1l, sc1l, g1l, sh2l, sc2l, g2l, sh3l, sc3l, g3l = ml
    sh1c, sc1c, g1c, sh2c, sc2c, g2c, sh3c, sc3c, g3c = mc
    xl = xl + 0.5 * g1l * swiglu(modulate(rmsnorm(xl, norm_w[0]), sh1l, sc1l), w_ffn1_in, w_ffn1_out)
    xc = xc + 0.5 * g1c * swiglu(modulate(rmsnorm(xc, norm_w[0]), sh1c, sc1c), w_ffn1_in, w_ffn1_out)
    hl = modulate(rmsnorm(xl, norm_w[1]), sh2l, sc2l)
    hc = modulate(rmsnorm(xc, norm_w[1]), sh2c, sc2c)
    mix_l, mix_c = token_mixer(hl, hc, w_in, q_norm_w, kv_norm_w, w_uq, w_ukv, conv_w, w_out,
                               tabs, not last)
    xl = xl + g2l * mix_l
    xl = xl + 0.5 * g3l * swiglu(modulate(rmsnorm(xl, norm_w[2]), sh3l, sc3l), w_ffn2_in, w_ffn2_out)
    if not last:
        xc = xc + g2c * mix_c
        xc = xc + 0.5 * g3c * swiglu(modulate(rmsnorm(xc, norm_w[2]), sh3c, sc3c), w_ffn2_in, w_ffn2_out)
    return xl, xc


def setup_inputs(seed: int = 0) -> dict:
    key = jax.random.key(seed)
    ks = jax.random.split(key, 20)
    f32 = jnp.float32

    def dense(k, shape, fan_in, gain=1.0):
        return jax.random.normal(k, shape, f32) * (gain * fan_in ** -0.5)

    def gains(k, shape):
        return 1.0 + 0.05 * jax.random.normal(k, shape, f32)

    return {
        "x": jax.random.normal(ks[0], (BATCH, SEQ, D_MODEL), f32),
        "c": jax.random.normal(ks[1], (BATCH, D_MODEL), f32),
        "ctx": jax.random.normal(ks[2], (BATCH, CTX_LEN, D_MODEL), f32),
        "c_ctx": jax.random.normal(ks[3], (D_MODEL,), f32),
        "w_ada": dense(ks[4], (DEPTH, D_MODEL, N_MOD * D_MODEL), D_MODEL, 0.5),
        "b_ada": 0.01 * jax.random.normal(ks[5], (DEPTH, N_MOD * D_MODEL), f32),
        "norm_w": gains(ks[6], (DEPTH, 3, D_MODEL)),
        "w_ffn1_in": dense(ks[7], (DEPTH, D_MODEL, 2 * D_FF), D_MODEL),
        "w_ffn1_out": dense(ks[8], (DEPTH, D_FF, D_MODEL), D_FF),
        "w_ffn2_in": dense(ks[9], (DEPTH, D_MODEL, 2 * D_FF), D_MODEL),
        "w_ffn2_out": dense(ks[10], (DEPTH, D_FF, D_MODEL), D_FF),
        "w_in": dense(ks[11], (DEPTH, D_MODEL, D_IN), D_MODEL),
        "q_norm_w": gains(ks[12], (DEPTH, Q_LORA_RANK)),
        "kv_norm_w": gains(ks[13], (DEPTH, KV_LORA_RANK)),
        "w_uq": dense(ks[14], (DEPTH, Q_LORA_RANK, MLA_HEADS * QK_HEAD_DIM), Q_LORA_RANK),
        "w_ukv": dense(ks[15], (DEPTH, KV_LORA_RANK, MLA_HEADS * (QK_NOPE_DIM + V_HEAD_DIM)), KV_LORA_RANK),
        "conv_w": dense(ks[16], (DEPTH, CONV_K, CONV_DIM), CONV_K),
        "w_out": dense(ks[17], (DEPTH, MIX_WIDTH, D_MODEL), MIX_WIDTH),
        "final_norm_w": gains(ks[18], (D_MODEL,)),
    }


def reference(x, c, ctx, c_ctx, w_ada, b_ada, norm_w, w_ffn1_in, w_ffn1_out, w_ffn2_in, w_ffn2_out,
              w_in, q_norm_w, kv_norm_w, w_uq, w_ukv, conv_w, w_out, final_norm_w):
    n_tokens = x.shape[1]
    rows = n_tokens // GRID_W
    tabs = axial_rope_tables(rows, x.dtype)
    xl, xc = x, ctx
    for i in range(DEPTH):
        last = i == DEPTH - 1
        ml = jnp.split((jax.nn.silu(c) @ w_ada[i] + b_ada[i])[:, None, :], N_MOD, axis=-1)
        mc = jnp.split(jax.nn.silu(c_ctx) @ w_ada[i] + b_ada[i], N_MOD, axis=-1)
        xl, xc = trunk_layer(xl, xc, ml, mc, norm_w[i], w_ffn1_in[i], w_ffn1_out[i],
                             w_ffn2_in[i], w_ffn2_out[i], w_in[i], q_norm_w[i], kv_norm_w[i],
                             w_uq[i], w_ukv[i], conv_w[i], w_out[i], tabs, last)
    return rmsnorm(xl, final_norm_w)
```

```python
import math
import contextlib
import numpy as np
import ml_dtypes
import concourse.bass as bass
import concourse.mybir as mybir
from concourse.bass_utils import run_bass_kernel_spmd

F32 = mybir.dt.float32
BF16 = mybir.dt.bfloat16
AF = mybir.ActivationFunctionType
ALU = mybir.AluOpType

P = 128
D = 1024
KC = 8
DFF = 2816
FC = 22
DEPTH = 2
NH = 8
L_TOK = 2048
C_TOK = 256
NTOK = L_TOK + C_TOK
NKEY = 8192 + C_TOK
NKC = NKEY // P
EPS = 1e-6
ATTN_SCALE = 1.0 / math.sqrt(96.0)
PAYW = L_TOK + 8
GROUPS = [(0, 4), (4, 4), (8, 4), (12, 4), (16, 3), (19, 3)]

PC_CVEC = 0
PC_BADA = 16
PC_NW = 160
PC_QNW = 208
PC_KVNW = 212
PC_CONVW = 214
PC_MASK = 238
PC_EPS = 270
NPAR = 272

BLOCKS = [(0, 512, 0), (512, 512, 0), (1024, 512, 0), (1536, 512, 0), (2048, 256, 1)]


class T:
    __slots__ = ("name", "w", "r")

    def __init__(self, name=""):
        self.name = name
        self.w = None
        self.r = {}


class Op:
    __slots__ = ("eng", "fn", "deps", "sig", "chan", "val", "ninc", "kind")


COMPUTE = ("pe", "act", "dve")
ENGS = ("pe", "act", "dve", "sp", "pool")


class Sched:
    def __init__(self):
        self.ops = {e: [] for e in ENGS}
        self.chan_last = {}
        self.chan_eng = {}
        self.chan_kind = {}

    def add(self, eng, name, kwargs, reads=(), writes=(), chan=None, ninc=1, kind=None):
        op = Op()
        op.eng = eng
        op.fn = (name, kwargs)
        op.sig = False
        op.val = 0
        op.ninc = ninc
        if chan is None:
            chan = eng
            kind = "c"
        elif kind is None:
            kind = "d"
        op.chan = chan
        op.kind = kind
        assert self.chan_eng.setdefault(chan, eng) == eng
        self.chan_kind[chan] = kind
        deps = set()
        for t in reads:
            if t.w is not None:
                deps.add(t.w)
        for t in writes:
            if t.w is not None:
                deps.add(t.w)
            deps.update(t.r.values())
        if eng == "pe":
            deps = {d for d in deps if d.eng != "pe"}
        op.deps = deps
        for t in reads:
            t.r[chan] = op
        for t in writes:
            t.w = op
            t.r = {}
        self.ops[eng].append(op)
        self.chan_last[chan] = op
        return op

    def barrier(self, engines=("pe", "act", "dve", "sp")):
        lasts = [op for ch, op in self.chan_last.items() if self.chan_eng[ch] in engines]
        for e in engines:
            op = Op()
            op.eng = e
            op.fn = None
            op.sig = False
            op.val = 0
            op.ninc = 0
            op.chan = e if e in COMPUTE else None
            op.kind = "c"
            op.deps = set(d for d in lasts if not (d.eng == e and e == "pe"))
            self.ops[e].append(op)

    def finalize(self):
        for e in ENGS:
            for op in self.ops[e]:
                for d in op.deps:
                    d.sig = True
                if op.kind == "d" and op.fn is not None:
                    op.sig = True
        cnt = {}
        for e in ENGS:
            for op in self.ops[e]:
                if op.sig and op.fn is not None:
                    inc = 16 * op.ninc if op.kind == "d" else 1
                    cnt[op.chan] = cnt.get(op.chan, 0) + inc
                    op.val = cnt[op.chan]
        return cnt

    def check_deadlock(self):
        done = set()
        pos = {e: 0 for e in ENGS}
        progress = True
        while progress:
            progress = False
            for e in ENGS:
                ops = self.ops[e]
                while pos[e] < len(ops):
                    op = ops[pos[e]]
                    if all((id(d) in done) for d in op.deps):
                        done.add(id(op))
                        pos[e] += 1
                        progress = True
                    else:
                        break
        stuck = {e: (pos[e], len(self.ops[e])) for e in ENGS if pos[e] < len(self.ops[e])}
        if stuck:
            msg = []
            for e, (p, n) in stuck.items():
                op = self.ops[e][p]
                msg.append((e, p, n, op.fn[0] if op.fn else None, [(d.eng, d.chan, d.fn[0] if d.fn else None) for d in op.deps if id(d) not in done]))
            raise RuntimeError("deadlock in schedule: %r" % (msg,))

    def emit(self, eng, handle, sems):
        seen = {}
        for op in self.ops[eng]:
            need = {}
            for d in op.deps:
                if d.fn is None:
                    continue
                if need.get(d.chan, 0) < d.val:
                    need[d.chan] = d.val
            for ch, v in need.items():
                if seen.get(ch, 0) < v:
                    handle.wait_ge(sems[ch], v)
                    seen[ch] = v
            if op.fn is None:
                continue
            name, kw = op.fn
            if name == "dma":
                ins = [handle.dma_start(**d) for d in kw]
            else:
                ins = getattr(handle, name)(**kw)
            if op.sig:
                if op.kind == "d":
                    lst = ins if isinstance(ins, (list, tuple)) else [ins]
                    assert len(lst) == op.ninc, (len(lst), op.ninc)
                    for i in lst:
                        i.then_inc(sems[op.chan], 16)
                else:
                    ins.then_inc(sems[op.chan], 1)


def _prod(s):
    r = 1
    for v in s:
        r *= v
    return r


class Arena:
    def __init__(self, tensor, nbytes):
        self.t = tensor
        self.nbytes = nbytes
        self.off = 0
        self.hi = 0

    def reset(self, off=0):
        self.off = off

    def alloc(self, shape, dt, parts=(0, P), at=None):
        es = 4 if dt == F32 else 2
        nel = _prod(shape)
        nb = (nel * es + 31) // 32 * 32
        off = self.off if at is None else at
        if at is None:
            self.off += nb
        assert off + nb <= self.nbytes, ("arena overflow", off, nb, self.nbytes)
        self.hi = max(self.hi, off + nb)
        a = self.t[parts[0]:parts[1], off // 4: (off + nb) // 4]
        if dt != F32:
            a = a.bitcast(dt)
        a = a[:, 0:nel]
        if len(shape) == 2:
            a = a.rearrange("p (a b) -> p a b", a=shape[0])
        elif len(shape) == 3:
            a = a.rearrange("p (a b c) -> p a b c", a=shape[0], b=shape[1])
        return a


def build(stop_after=None):
    nc = bass.Bass("TRN2", target_bir_lowering=False)

    def din(name, shape, dt=F32):
        return nc.dram_tensor(name, list(shape), dt, kind="ExternalInput").ap()

    x_d = din("x", [L_TOK, D])
    ctx_d = din("ctx", [C_TOK, D])
    par_d = din("par", [P, NPAR])
    wada_d = din("w_ada", [DEPTH, D, 9 * D])
    wfi_d = [din("w_ffn1_in", [DEPTH, D, 2 * DFF]), din("w_ffn2_in", [DEPTH, D, 2 * DFF])]
    wfo_d = [din("w_ffn1_out", [DEPTH, DFF, D]), din("w_ffn2_out", [DEPTH, DFF, D])]
    win_d = din("w_in_r", [DEPTH, D, 2048])
    wuq_d = din("w_uq_r", [DEPTH, 256, 1536])
    wuk_d = din("w_uk_r", [DEPTH, 128, 512])
    wuv_d = din("w_uv_r", [DEPTH, 128, 512])
    wout_d = din("w_out", [DEPTH, D, D])
    fnw_d = din("fnw_bc", [P, D])
    rope_d = din("rope", [32, 2, L_TOK], BF16)
    ident_d = din("ident", [P, P])
    out_d = nc.dram_tensor("out", [L_TOK, D], F32, kind="ExternalOutput").ap()
    pay_d = [nc.dram_tensor("pay%d" % l, [160, PAYW], BF16) for l in range(DEPTH)]
    gath_d = [nc.dram_tensor("gath%d" % l, [4 * 160, PAYW], BF16) for l in range(DEPTH)]

    S = Sched()
    es = contextlib.ExitStack()
    with es:
        def sb(name, shape, dt):
            return es.enter_context(nc.sbuf_tensor("sb_" + name, list(shape), dt))

        xT = sb("xT", [P, KC, NTOK], F32)
        slotA = sb("slotA", [P, 12288], BF16)
        slotB = sb("slotB", [P, 12288], BF16)
        slotC = sb("slotC", [P, 4096], BF16)
        par = sb("par", [P, NPAR], F32)
        modT = sb("modT", [P, DEPTH, 72, 2], F32)
        drv = sb("drv", [P, DEPTH * 3 * 2 * KC * 2], F32)
        ident = sb("ident", [P, P], F32)
        ones_bf = sb("ones_bf", [P, P], BF16)
        ones_f = sb("ones_f", [P, 64], F32)
        scT = sb("scT", [P, KC, 2], BF16)
        ARENA_BYTES = 77696
        arena_t = sb("arena", [P, ARENA_BYTES // 4], F32)
        AR = Arena(arena_t, ARENA_BYTES)
        banks = [es.enter_context(nc.psum_tensor("bank%d" % i, [P, 512], F32)) for i in range(8)]
        bank_t = [T("bank%d" % i) for i in range(8)]

        def mm(out, lhsT, rhs, start, stop, reads, writes):
            S.add("pe", "matmul", dict(out=out, lhsT=lhsT, rhs=rhs, start=start, stop=stop), reads, writes)

        def tr(out, in_, reads, writes):
            S.add("pe", "transpose", dict(out=out, in_=in_, identity=ident[:, :]), reads, writes)

        def act(out, in_, func, reads, writes, **kw):
            S.add("act", "activation", dict(out=out, in_=in_, func=func, **kw), reads, writes)

        def tt(out, in0, in1, op, reads, writes):
            S.add("dve", "tensor_tensor", dict(out=out, in0=in0, in1=in1, op=op), reads, writes)

        def stt(out, in0, scalar, in1, op0, op1, reads, writes):
            S.add("dve", "scalar_tensor_tensor", dict(out=out, in0=in0, scalar=scalar, in1=in1, op0=op0, op1=op1), reads, writes)

        def ts(out, in0, scalar1, op0, reads, writes):
            S.add("dve", "tensor_scalar", dict(out=out, in0=in0, scalar1=scalar1, scalar2=None, op0=op0), reads, writes)

        def cp(out, in_, reads, writes):
            S.add("dve", "tensor_copy", dict(out=out, in_=in_), reads, writes)

        def recip(out, in_, reads, writes):
            S.add("dve", "reciprocal", dict(out=out, in_=in_), reads, writes)

        def memset(ap, v, writes):
            S.add("dve", "memset", dict(ap=ap, constant=v), [], writes)

        def dma(eng, pairs, reads, writes, chan):
            S.add(eng, "dma", [dict(out=o, in_=i) for (o, i) in pairs], reads, writes, chan=chan, ninc=len(pairs))

        cp_i = [0]

        def evac_copy(out_ap, in_ap, reads, writes):
            cp_i[0] += 1
            if cp_i[0] % 2:
                act(out_ap, in_ap, AF.Copy, reads, writes)
            else:
                cp(out_ap, in_ap, reads, writes)

        def drv_ap(l, s, which, c, var):
            i = ((((l * 3 + s) * 2 + which) * KC) + c) * 2 + var
            return drv[:, i:i + 1]

        def drv_vec(l, s, which, var):
            i0 = (((l * 3 + s) * 2 + which) * KC) * 2 + var
            return drv[:, i0:i0 + 2 * KC - 1:2]

        x_t = [[T("x%d_%d" % (c, b)) for b in range(5)] for c in range(KC)]
        slot_t = {"A": T("slotA"), "B": T("slotB"), "C": T("slotC")}
        par_t = T("par")
        mod_t = T("mod")
        drv_t = T("drv")
        const_t = T("const")
        sc_t = T("scT")
        out_tiles = [T("out%d" % i) for i in range(16)]

        class Rot:
            def __init__(self, items):
                self.items = items
                self.i = 0

            def next(self):
                it = self.items[self.i % len(self.items)]
                self.i += 1
                return it

        def bankpool(ids):
            return Rot([(banks[i], bank_t[i]) for i in ids])

        dma("sp", [(par[:, :], par_d[:, :])], [], [par_t], "d_par")
        dma("sp", [(ident[:, :], ident_d[:, :])], [], [const_t], "d_const")
        memset(ones_bf[:, :], 1.0, [const_t])
        memset(ones_f[:, :], 0.0, [const_t])
        memset(ones_f[64:65, :], 1.0, [const_t])

        AR.reset()
        xs = [AR.alloc([D], F32) for _ in range(3)]
        xs_t = [T("xs%d" % i) for i in range(3)]
        pp = bankpool([0, 1, 2, 3])
        for tk in range(NTOK // P):
            st = xs[tk % 3]
            stt_ = xs_t[tk % 3]
            if tk < 16:
                src = x_d[tk * P:(tk + 1) * P, :]
            else:
                src = ctx_d[(tk - 16) * P:(tk - 15) * P, :]
            dma("sp", [(st[:, :], src)], [], [stt_], "d_xs%d" % (tk % 3))
            bi = min(tk // 4, 4)
            for hf in range(2):
                bk, bt = pp.next()
                for c4 in range(4):
                    c = hf * 4 + c4
                    tr(bk[:, c4 * P:(c4 + 1) * P], st[:, c * P:(c + 1) * P], [stt_, const_t], [bt])
                evac_copy(xT[:, hf * 4:hf * 4 + 4, tk * P:(tk + 1) * P], bk[:, :].rearrange("p (a b) -> p a b", a=4),
                          [bt], [x_t[hf * 4 + c4][bi] for c4 in range(4)])

        act(scT[:, :, :], par[:, PC_CVEC:PC_CVEC + 16].rearrange("p (k n) -> p k n", n=2), AF.Silu, [par_t], [sc_t])
        slots = {"A": slotA, "B": slotB}
        slot_seq = ["A"]

        def next_slot():
            sn = slot_seq[0]
            slot_seq[0] = "B" if sn == "A" else "A"
            return sn, slots[sn]

        for l in range(DEPTH):
            bk, bt = banks[7], bank_t[7]
            for g in range(6):
                sn, sl = next_slot()
                wv = sl[:, 0:KC * 1536].rearrange("p (k n) -> p k n", k=KC)
                dma("pool", [(wv, wada_d[l, :, g * 1536:(g + 1) * 1536].rearrange("(k p) n -> p k n", p=P))], [], [slot_t[sn]], "d_slot" + sn)
                for jl in range(12):
                    j = g * 12 + jl
                    for k in range(KC):
                        mm(bk[:, j * 2:j * 2 + 2], wv[:, k, jl * P:(jl + 1) * P], scT[:, k, :], k == 0, k == KC - 1, [slot_t[sn], sc_t], [bt])
            for n in range(2):
                tt(modT[:, l, :, n], bk[:, n:143 + n:2], par[:, PC_BADA + l * 72:PC_BADA + (l + 1) * 72], ALU.add, [bt, par_t], [mod_t])
            for s in range(3):
                for n in range(2):
                    nw = par[:, PC_NW + (l * 3 + s) * 8:PC_NW + (l * 3 + s) * 8 + 8]
                    stt(drv_vec(l, s, 0, n), modT[:, l, (3 * s + 1) * 8:(3 * s + 1) * 8 + 8, n], 1.0, nw, ALU.add, ALU.mult, [mod_t, par_t], [drv_t])
                    gsc = 1.0 if s == 1 else 0.5
                    ts(drv_vec(l, s, 1, n), modT[:, l, (3 * s + 2) * 8:(3 * s + 2) * 8 + 8, n], gsc, ALU.mult, [mod_t], [drv_t])

        def shift_ap(l, s, c, var):
            return modT[:, l, 3 * s * 8 + c, var:var + 1]

        eps_ap = par[:, PC_EPS:PC_EPS + 1]

        def norm_mod(l, s, bi, h_aps, h_tiles, tmp):
            t0, n, var = BLOCKS[bi]
            sqr, sq_t, s_ap, s_t, r_ap, r_t, tt_, tt_t, ss_pool = tmp
            bk, bt = ss_pool.next()
            for c in range(KC):
                q, qt = sqr[c % len(sqr)], sq_t[c % len(sqr)]
                act(q[:, :n], xT[:, c, t0:t0 + n], AF.Square, [x_t[c][bi]], [qt])
                mm(bk[:, :n], ones_bf[:, :], q[:, :n], c == 0, c == KC - 1, [qt, const_t], [bt])
            act(s_ap[:, :n], bk[:, :n], AF.Sqrt, [bt, par_t], [s_t], bias=eps_ap, scale=1.0 / D)
            recip(r_ap[:, :n], s_ap[:, :n], [s_t], [r_t])
            for c in range(KC):
                tb, tbt = tt_[c % 2], tt_t[c % 2]
                stt(tb[:, :n], xT[:, c, t0:t0 + n], drv_ap(l, s, 0, c, var), r_ap[:, :n], ALU.mult, ALU.mult, [x_t[c][bi], drv_t, r_t], [tbt])
                act(h_aps[c], tb[:, :n], AF.Identity, [tbt, mod_t], [h_tiles[c]], bias=shift_ap(l, s, c, var), scale=1.0)

        def ffn(l, f, bis, first_slot=None):
            s = 0 if f == 0 else 2
            if first_slot is not None:
                slot_seq[0] = first_slot
            AR.reset()
            hT = AR.alloc([KC, NTOK], BF16)
            h_t = [[T("h%d_%d" % (c, b)) for b in range(5)] for c in range(KC)]
            actb = [AR.alloc([4, 512], BF16) for _ in range(2)]
            act_t = [[T("act%d_%d" % (i, j)) for j in range(4)] for i in range(2)]
            sqr = [AR.alloc([512], BF16) for _ in range(3)]
            sq_t = [T("sq%d" % i) for i in range(3)]
            s_ap = AR.alloc([512], F32); s_t = T("s")
            r_ap = AR.alloc([512], F32); r_t = T("r")
            tt_ = [AR.alloc([512], F32) for _ in range(2)]
            tt_t = [T("tt0"), T("tt1")]
            sg = [AR.alloc([512], F32) for _ in range(2)]
            sg_t = [T("sg0"), T("sg1")]
            tmp = (sqr, sq_t, s_ap, s_t, r_ap, r_t, tt_, tt_t, bankpool([6, 7]))
            def do_norm(bi):
                t0, n, var = BLOCKS[bi]
                norm_mod(l, s, bi, [hT[:, c, t0:t0 + n] for c in range(KC)], [h_t[c][bi] for c in range(KC)], tmp)
            do_norm(bis[0])
            gu_pool = bankpool([0, 1, 2, 3])
            o_pool = bankpool([4, 5])
            ai = 0
            sgi = 0
            for (j0, ng) in GROUPS:
                sn, sl = next_slot()
                wg = sl[:, 0:KC * 512].rearrange("p (k n) -> p k n", k=KC)
                wu = sl[:, 4096:4096 + KC * 512].rearrange("p (k n) -> p k n", k=KC)
                wo = sl[:, 8192:8192 + 4 * D].rearrange("p (j n) -> p j n", j=4)
                st = slot_t[sn]
                dma("pool", [(wg[:, :, 0:ng * P], wfi_d[f][l, :, j0 * P:(j0 + ng) * P].rearrange("(k p) n -> p k n", p=P)),
                             (wu[:, :, 0:ng * P], wfi_d[f][l, :, DFF + j0 * P:DFF + (j0 + ng) * P].rearrange("(k p) n -> p k n", p=P)),
                             (wo[:, 0:ng, :], wfo_d[f][l, j0 * P:(j0 + ng) * P, :].rearrange("(j p) n -> p j n", p=P))],
                    [], [st], "d_slot" + sn)

                def emit_out(bi, ab, abt):
                    t0, n, var = BLOCKS[bi]
                    for m in range(KC):
                        bk, bt = o_pool.next()
                        for jl in range(ng):
                            mm(bk[:, :n], wo[:, jl, m * P:(m + 1) * P], ab[:, jl, :n], jl == 0, jl == ng - 1, [st, abt[jl]], [bt])
                        stt(xT[:, m, t0:t0 + n], bk[:, :n], drv_ap(l, s, 1, m, var), xT[:, m, t0:t0 + n], ALU.mult, ALU.add,
                            [bt, drv_t, x_t[m][bi]], [x_t[m][bi]])
                pending = None
                for bi in bis:
                    t0, n, var = BLOCKS[bi]
                    ab, abt = actb[ai % 2], act_t[ai % 2]
                    ai += 1
                    for jl in range(ng):
                        gb_, gbt = gu_pool.next()
                        ub_, ubt = gu_pool.next()
                        for k in range(KC):
                            mm(gb_[:, :n], wg[:, k, jl * P:(jl + 1) * P], hT[:, k, t0:t0 + n], k == 0, k == KC - 1, [st, h_t[k][bi]], [gbt])
                        for k in range(KC):
                            mm(ub_[:, :n], wu[:, k, jl * P:(jl + 1) * P], hT[:, k, t0:t0 + n], k == 0, k == KC - 1, [st, h_t[k][bi]], [ubt])
                        sgb, sgt = sg[sgi % 2], sg_t[sgi % 2]
                        sgi += 1
                        act(sgb[:, :n], gb_[:, :n], AF.Silu, [gbt], [sgt])
                        tt(ab[:, jl, :n], ub_[:, :n], sgb[:, :n], ALU.mult, [ubt, sgt], [abt[jl]])
                        if j0 == 0 and jl == 0 and bis.index(bi) + 1 < len(bis):
                            do_norm(bis[bis.index(bi) + 1])
                    if pending is not None:
                        emit_out(*pending)
                    pending = (bi, ab, abt)
                emit_out(*pending)
            S.barrier()

        def mixer(l):
            last = (l == DEPTH - 1)
            AR.reset()
            cqn = AR.alloc([2, NTOK], BF16); cqn_t = [[T("cqn%d_%d" % (c, b)) for b in range(5)] for c in range(2)]
            tab_off = AR.off
            tab = AR.alloc([2, L_TOK], BF16, parts=(64, 96)); tab_t = T("tab")
            ckv_ctx = AR.alloc([C_TOK], BF16); ckvc_t = T("ckvctx")
            kpe_ctx = AR.alloc([C_TOK], BF16, parts=(64, 96)); kpec_t = T("kpectx")
            hg = AR.alloc([4, 8], BF16); hg_t = T("hg")
            edge = AR.alloc([4, 4, 4], F32); edge_t = T("edge")
            nb = AR.alloc([4, 4, 2], F32); nb_t = T("nb")
            hsel = AR.alloc([4, 8], F32); hsel_t = T("hsel")
            hacc = AR.alloc([8], F32); hacc_t = T("hacc")
            base_off = AR.off
            conv = AR.alloc([4, NTOK], BF16); conv_end = AR.off; conv_t = [[T("conv%d_%d" % (c, b)) for b in range(5)] for c in range(4)]
            h2 = AR.alloc([KC, 512], BF16); h2_t = [T("h2_%d" % c) for c in range(KC)]
            gbb = AR.alloc([4, 512], BF16); gbb_t = [T("gbb%d" % c) for c in range(4)]
            ub = AR.alloc([4, 512], BF16); ub_t = [T("ub%d" % c) for c in range(4)]
            gcs = [AR.alloc([512], F32) for _ in range(2)]; gcs_t = [T("gcs0"), T("gcs1")]
            sqr = [AR.alloc([512], BF16) for _ in range(3)]; sq_t = [T("sq%d" % i) for i in range(3)]
            s_ap = AR.alloc([512], F32); s_t = T("s")
            r_ap = AR.alloc([512], F32); r_t = T("r")
            tt_ = [AR.alloc([512], F32) for _ in range(2)]; tt_t = [T("tt0"), T("tt1")]
            yb = tt_; yb_t = tt_t
            s2 = [s_ap, s_ap]; s2_t = [s_t, s_t]
            r2 = [r_ap, r_ap]; r2_t = [r_t, r_t]
            rt1 = AR.alloc([512], F32, parts=(64, 96)); rt1_t = T("rt1")
            rt2 = AR.alloc([512], F32, parts=(64, 96)); rt2_t = T("rt2")
            stg = [AR.alloc([512], BF16) for _ in range(2)]; stg_t = [T("stg0"), T("stg1")]
            kst = [AR.alloc([512], BF16, parts=(64, 96)) for _ in range(2)]; kst_t = [T("kst0"), T("kst1")]
            pay_tiles = []
            gath_t = T("gath")

            winA = slotA[:, 0:KC * 1536].rearrange("p (k n) -> p k n", k=KC)
            winB = slotB[:, 0:KC * 512].rearrange("p (k n) -> p k n", k=KC)
            woc = slotB[:, 4096:4096 + 4 * D].rearrange("p (j n) -> p j n", j=4)
            wuq = slotC[:, 0:2 * 1536].rearrange("p (k n) -> p k n", k=2)
            wuk = slotC[:, 3072:3584]
            wuv = slotC[:, 3584:4096]
            dma("pool", [(winA, win_d[l, :, 0:1536].rearrange("(k p) n -> p k n", p=P))], [], [slot_t["A"]], "d_slotA")
            dma("pool", [(winB, win_d[l, :, 1536:2048].rearrange("(k p) n -> p k n", p=P)),
                         (woc, wout_d[l, 512:1024, :].rearrange("(j p) n -> p j n", p=P))], [], [slot_t["B"]], "d_slotB")
            dma("pool", [(wuq, wuq_d[l, :, :].rearrange("(k p) n -> p k n", p=P)), (wuk, wuk_d[l, :, :]), (wuv, wuv_d[l, :, :])],
                [], [slot_t["C"]], "d_slotC")
            dma("sp", [(tab, rope_d[:, :, :])], [], [tab_t], "d_tab")

            def win_ap(k, c0, c1):
                if c1 <= 1536:
                    return winA[:, k, c0:c1], slot_t["A"]
                assert c0 >= 1536
                return winB[:, k, c0 - 1536:c1 - 1536], slot_t["B"]

            gen_pool = bankpool([0, 1, 2, 3])
            hold_pool = bankpool([4, 5, 6])
            ss_pool = bankpool([7])
            nm_tmp = (sqr, sq_t, s_ap, s_t, r_ap, r_t, tt_, tt_t, ss_pool)

            def cw(k, c):
                i = PC_CONVW + (l * 3 + k) * 4 + c
                return par[:, i:i + 1]

            for bi in [0, 1, 2, 3, 4]:
                t0, n, var = BLOCKS[bi]
                ctx_only_kv = (var == 1 and last)
                if bi == 0:
                    norm_mod(l, 1, 0, [h2[:, c, :n] for c in range(KC)], h2_t, nm_tmp)

                def proj(bk, bt, c0, c1, M):
                    for k in range(KC):
                        w, wt = win_ap(k, c0, c1)
                        mm(bk[0:M, :n], w, h2[:, k, :n], k == 0, k == KC - 1, [wt, h2_t[k]], [bt])

                def rms_small(pss, nfeat, normw_col, outs, out_tiles_):
                    bk2, bt2 = ss_pool.next()
                    for i, (bk, bt) in enumerate(pss):
                        q, qt = sqr[i % 3], sq_t[i % 3]
                        act(q[:, :n], bk[:, :n], AF.Square, [bt], [qt])
                        mm(bk2[:, :n], ones_bf[:, :], q[:, :n], i == 0, i == len(pss) - 1, [qt, const_t], [bt2])
                    ix = 0 if len(pss) == 2 else 1
                    act(s2[ix][:, :n], bk2[:, :n], AF.Sqrt, [bt2, par_t], [s2_t[ix]], bias=eps_ap, scale=1.0 / nfeat)
                    recip(r2[ix][:, :n], s2[ix][:, :n], [s2_t[ix]], [r2_t[ix]])
                    for i, (bk, bt) in enumerate(pss):
                        stt(outs[i], bk[:, :n], par[:, normw_col + i:normw_col + i + 1], r2[ix][:, :n], ALU.mult, ALU.mult,
                            [bt, par_t, r2_t[ix]], [out_tiles_[i]])

                kv = hold_pool.next()
                proj(kv[0], kv[1], 256, 384, P)
                if var == 0:
                    sg_, sgt_ = stg[bi % 2], stg_t[bi % 2]
                    rms_small([kv], 128, PC_KVNW + l, [sg_[:, :n]], [sgt_])
                    pt_ = T("pay")
                    pay_tiles.append(pt_)
                    dma("sp", [(pay_d[l][0:128, t0:t0 + n], sg_[:, :n])], [sgt_], [pt_], "d_pay%d" % (bi % 2))
                else:
                    rms_small([kv], 128, PC_KVNW + l, [ckv_ctx[:, :n]], [ckvc_t])
                ka = gen_pool.next()
                proj(ka[0], ka[1], 1920, 2016, 96)
                if var == 0:
                    kb_ = gen_pool.next()
                    proj(kb_[0], kb_[1], 1952, 2048, 96)
                    ks_, kst__ = kst[bi % 2], kst_t[bi % 2]
                    tt(rt1[:, :n], ka[0][64:96, :n], tab[:, 0, t0:t0 + n], ALU.mult, [ka[1], tab_t], [rt1_t])
                    tt(rt2[:, :n], kb_[0][64:96, :n], tab[:, 1, t0:t0 + n], ALU.mult, [kb_[1], tab_t], [rt2_t])
                    tt(ks_[:, :n], rt1[:, :n], rt2[:, :n], ALU.add, [rt1_t, rt2_t], [kst__])
                    pt_ = T("payk")
                    pay_tiles.append(pt_)
                    pairs = [(pay_d[l][128:160, t0:t0 + n], ks_[:, :n])]
                    if bi == 3:
                        pairs.append((pay_d[l][128:160, L_TOK:L_TOK + 8], ks_[:, 0:8]))
                    dma("sp", pairs, [kst__], [pt_], "d_payk%d" % (bi % 2))
                else:
                    act(kpe_ctx[:, :n], ka[0][64:96, :n], AF.Copy, [ka[1]], [kpec_t])
                if ctx_only_kv:
                    continue
                cq = [hold_pool.next(), hold_pool.next()]
                for i in range(2):
                    proj(cq[i][0], cq[i][1], i * P, (i + 1) * P, P)
                rms_small(cq, 256, PC_QNW + l * 2, [cqn[:, i, t0:t0 + n] for i in range(2)], [cqn_t[0][bi], cqn_t[1][bi]])
                for c in range(4):
                    g_ = gen_pool.next()
                    proj(g_[0], g_[1], 384 + c * P, 384 + (c + 1) * P, P)
                    act(gbb[:, c, :n], g_[0][:, :n], AF.Copy, [g_[1]], [gbb_t[c]])
                for c in range(4):
                    g_ = gen_pool.next()
                    proj(g_[0], g_[1], 896 + c * P, 896 + (c + 1) * P, P)
                    act(gcs[c % 2][:, :n], g_[0][:, :n], AF.Copy, [g_[1]], [gcs_t[c % 2]])
                    v_ = gen_pool.next()
                    proj(v_[0], v_[1], 1408 + c * P, 1408 + (c + 1) * P, P)
                    tt(ub[:, c, :n], v_[0][:, :n], gcs[c % 2][:, :n], ALU.mult, [v_[1], gcs_t[c % 2]], [ub_t[c]])
                if bi + 1 < 5:
                    n_nx = BLOCKS[bi + 1][1]
                    norm_mod(l, 1, bi + 1, [h2[:, c, :n_nx] for c in range(KC)], h2_t, nm_tmp)
                for c in range(4):
                    y, yt = yb[c % 2], yb_t[c % 2]
                    ts(y[:, :n], ub[:, c, :n], cw(1, c), ALU.mult, [ub_t[c], par_t], [yt])
                    stt(y[:, 1:n], ub[:, c, 0:n - 1], cw(0, c), y[:, 1:n], ALU.mult, ALU.add, [ub_t[c], par_t, yt], [yt])
                    stt(y[:, 0:n - 1], ub[:, c, 1:n], cw(2, c), y[:, 0:n - 1], ALU.mult, ALU.add, [ub_t[c], par_t, yt], [yt])
                    tt(conv[:, c, t0:t0 + n], y[:, :n], gbb[:, c, :n], ALU.mult, [yt, gbb_t[c]], [conv_t[c][bi]])
                if var == 0:
                    cp(edge[:, :, bi, 0:2], ub[:, :, 0:n:n - 1], ub_t, [edge_t])
                    cp(edge[:, :, bi, 2:4], gbb[:, :, 0:n:n - 1], gbb_t, [edge_t])
            hst = stg[0]
            hstv = hst[:, 0:8].rearrange("p (c e) -> p c e", e=2)
            cp(hstv[:, :, 0], edge[:, :, 0, 0], [edge_t], [stg_t[0]])
            cp(hstv[:, :, 1], edge[:, :, 3, 1], [edge_t, stg_t[0]], [stg_t[0]])
            pt_ = T("payh")
            pay_tiles.append(pt_)
            dma("sp", [(pay_d[l][0:128, L_TOK:L_TOK + 8], hst[:, 0:8])], [stg_t[0]], [pt_], "d_pay0")
            if stop_after != ("inproj_nocc", l):
              S.add("pool", "collective_compute", dict(kind="AllGather", op=ALU.bypass, replica_groups=[[0, 1, 2, 3], [4, 5, 6, 7]],
                                                      ins=[pay_d[l].ap().opt()], outs=[gath_d[l].ap().opt()]),
                  pay_tiles, [gath_t], chan="cc", kind="cc")
            S.barrier()
            if stop_after == ("cc", l):
                dma("sp", [(hg, gath_d[l].ap().rearrange("(j r) n -> r j n", r=160)[0:128, :, L_TOK:L_TOK + 8])], [gath_t], [hg_t], "d_hg")
                S.barrier()
                return True
            if stop_after in (("inproj", l), ("inproj_nocc", l)):
                return True

            AR.reset(base_off)
            Vt = AR.alloc([NKC, 65], BF16); V_t = [T("V%d" % i) for i in range(9)]; Vones_t = T("Vones")
            Pt = [AR.alloc([512], BF16) for _ in range(3)]; P_t = [T("P%d" % i) for i in range(3)]
            qt = [AR.alloc([512], BF16, parts=(0, 96)) for _ in range(2)]; q_t = [T("q0"), T("q1")]
            osb = [AR.alloc([512], F32, parts=(0, 65)) for _ in range(2)]; osb_t = [T("osb0"), T("osb1")]
            qr1 = AR.alloc([512], F32, parts=(64, 96)); qr1_t = T("qr1")
            qr2 = AR.alloc([512], F32, parts=(64, 96)); qr2_t = T("qr2")
            assert AR.off >= conv_end
            Kt = AR.alloc([NKEY], BF16, parts=(0, 96)); K_t = [T("K%d" % i) for i in range(17)]
            att = [AR.alloc([512], BF16, parts=(0, 64), at=tab_off + i * 1024) for i in range(2)]; att_t = [T("att0"), T("att1")]
            woh = [AR.alloc([D], BF16, parts=(0, 64), at=tab_off + 2048 + i * 2048) for i in range(2)]; woh_t = [T("woh0"), T("woh1")]
            ckv_all = slotA[:, 0:NKEY]
            sA = slot_t["A"]
            ckva_t = [T("ckva%d" % i) for i in range(5)]

            Kpe_t = [T("kpe%d" % i) for i in range(5)]
            for j in range(4):
                dma("sp", [(ckv_all[:, C_TOK + j * L_TOK:C_TOK + (j + 1) * L_TOK], gath_d[l][j * 160:j * 160 + 128, 0:L_TOK])],
                    [gath_t], [ckva_t[1 + j]], "d_ga%d" % j)
                dma("sp", [(Kt[64:96, C_TOK + j * L_TOK:C_TOK + (j + 1) * L_TOK], gath_d[l][j * 160 + 128:j * 160 + 160, 0:L_TOK])],
                    [gath_t], [Kpe_t[1 + j]], "d_gk%d" % j)
            dma("sp", [(hg, gath_d[l].ap().rearrange("(j r) n -> r j n", r=160)[0:128, :, L_TOK:L_TOK + 8])], [gath_t], [hg_t], "d_hg")
            cp(ckv_all[:, 0:C_TOK], ckv_ctx[:, :], [ckvc_t], [ckva_t[0]])
            act(Kt[64:96, 0:C_TOK], kpe_ctx[:, :], AF.Copy, [kpec_t], [Kpe_t[0]])

            if stop_after == ("post1", l):
                S.barrier()
                return True
            tt(hsel, hg, par[:, PC_MASK:PC_MASK + 32].rearrange("p (j n) -> p j n", j=4), ALU.mult, [hg_t, par_t], [hsel_t])
            tt(hacc[:, :], hsel[:, 0, :], hsel[:, 1, :], ALU.add, [hsel_t], [hacc_t])
            tt(hacc[:, :], hacc[:, :], hsel[:, 2, :], ALU.add, [hsel_t, hacc_t], [hacc_t])
            tt(hacc[:, :], hacc[:, :], hsel[:, 3, :], ALU.add, [hsel_t, hacc_t], [hacc_t])
            hv = hacc[:, :].rearrange("p (c e) -> p c e", e=2)
            cp(nb[:, :, 1:4, 0], edge[:, :, 0:3, 1], [edge_t], [nb_t])
            cp(nb[:, :, 0, 0], hv[:, :, 1], [hacc_t, nb_t], [nb_t])
            cp(nb[:, :, 0:3, 1], edge[:, :, 1:4, 0], [edge_t, nb_t], [nb_t])
            cp(nb[:, :, 3, 1], hv[:, :, 0], [hacc_t, nb_t], [nb_t])
            tt(nb[:, :, :, :], nb[:, :, :, :], edge[:, :, :, 2:4], ALU.mult, [edge_t, nb_t], [nb_t])
            for c in range(4):
                for sd in range(2):
                    cv = conv[:, c, sd * 511:L_TOK:512]
                    cts = [conv_t[c][b] for b in range(4)]
                    stt(cv, nb[:, c, :, sd], cw(2 * sd, c), cv, ALU.mult, ALU.add, [nb_t, par_t] + cts, cts)
            if stop_after == ("post2", l):
                S.barrier()
                return True
            o_pool = bankpool([6, 7])
            for bi in ([0, 1, 2, 3] if last else [0, 1, 2, 3, 4]):
                t0, n, var = BLOCKS[bi]
                for m in range(KC):
                    bk, bt = o_pool.next()
                    for c in range(4):
                        mm(bk[:, :n], woc[:, c, m * P:(m + 1) * P], conv[:, c, t0:t0 + n], c == 0, c == 3, [slot_t["B"], conv_t[c][bi]], [bt])
                    stt(xT[:, m, t0:t0 + n], bk[:, :n], drv_ap(l, 1, 1, m, var), xT[:, m, t0:t0 + n], ALU.mult, ALU.add,
                        [bt, drv_t, x_t[m][bi]], [x_t[m][bi]])
            S.barrier()
            if stop_after == ("conv", l):
                return True
            memset(Vt[:, :, 64:65], 1.0, [Vones_t])

            s_pool = bankpool([0, 1, 2])
            oacc_pool = bankpool([3, 4])
            e_pool = bankpool([5])
            kv_pool = bankpool([0, 1, 2, 5])
            cnt_ = {"q": 0, "p": 0, "o": 0, "a": 0}
            qblocks = [0, 1, 2, 3] if last else [0, 1, 2, 3, 4]
            items = [(h, bi) for h in range(NH) for bi in qblocks]

            def expand_kv(h):
                for kb in range(17):
                    k0 = kb * 512
                    kn = min(512, NKEY - k0)
                    bk, bt = kv_pool.next()
                    if kb == 0:
                        src_t = [ckva_t[0], ckva_t[1]]
                    else:
                        src_t = [ckva_t[1 + (k0 - C_TOK) // L_TOK], ckva_t[1 + min(3, (k0 + kn - 1 - C_TOK) // L_TOK)]]
                    mm(bk[0:64, :kn], wuk[:, h * 64:(h + 1) * 64], ckv_all[:, k0:k0 + kn], True, True, [slot_t["C"], sA] + src_t, [bt])
                    evac_copy(Kt[0:64, k0:k0 + kn], bk[0:64, :kn], [bt], [K_t[kb]])
                for kg in range(9):
                    c0 = kg * 8
                    cn = min(8, NKC - c0)
                    bk, bt = kv_pool.next()
                    for i in range(cn):
                        kc = c0 + i
                        mm(bk[:, i * 64:(i + 1) * 64], ckv_all[:, kc * P:(kc + 1) * P], wuv[:, h * 64:(h + 1) * 64], True, True,
                           [slot_t["C"], sA] + ckva_t, [bt])
                    evac_copy(Vt[:, c0:c0 + cn, 0:64], bk[:, 0:cn * 64].rearrange("p (a b) -> p a b", b=64), [bt], [V_t[kg]])
                wo_h, wo_ht = woh[h % 2], woh_t[h % 2]
                dma("pool", [(wo_h[:, :], wout_d[l, h * 64:(h + 1) * 64, :])], [], [wo_ht], "d_woh%d" % (h % 2))

            def q_stages(h, bi):
                t0, n, var = BLOCKS[bi]
                qb_, qbt = qt[cnt_["q"] % 2], q_t[cnt_["q"] % 2]
                cnt_["q"] += 1

                def st_a():
                    qa = e_pool.next()
                    for k in range(2):
                        mm(qa[0][0:96, :n], wuq[:, k, h * 192:h * 192 + 96], cqn[:, k, t0:t0 + n], k == 0, k == 1, [slot_t["C"], cqn_t[k][bi]], [qa[1]])
                    act(qb_[0:64, :n], qa[0][0:64, :n], AF.Copy, [qa[1]], [qbt])
                    if var == 0:
                        tt(qr1[:, :n], qa[0][64:96, :n], tab[:, 0, t0:t0 + n], ALU.mult, [qa[1], tab_t], [qr1_t])
                    else:
                        act(qb_[64:96, :n], qa[0][64:96, :n], AF.Copy, [qa[1], qbt], [qbt])

                def st_b():
                    if var == 0:
                        qs = e_pool.next()
                        for k in range(2):
                            mm(qs[0][0:96, :n], wuq[:, k, h * 192 + 96:h * 192 + 192], cqn[:, k, t0:t0 + n], k == 0, k == 1, [slot_t["C"], cqn_t[k][bi]], [qs[1]])
                        tt(qr2[:, :n], qs[0][64:96, :n], tab[:, 1, t0:t0 + n], ALU.mult, [qs[1], tab_t], [qr2_t])
                        tt(qb_[64:96, :n], qr1[:, :n], qr2[:, :n], ALU.add, [qr1_t, qr2_t, qbt], [qbt])
                return [st_a, st_b], qb_, qbt

            def tail_stages(h, bi, ob, obt):
                t0, n, var = BLOCKS[bi]
                wo_h, wo_ht = woh[h % 2], woh_t[h % 2]
                os_, ost = osb[cnt_["o"] % 2], osb_t[cnt_["o"] % 2]
                cnt_["o"] += 1
                at_, att_ = att[cnt_["a"] % 2], att_t[cnt_["a"] % 2]
                cnt_["a"] += 1

                def st_a():
                    act(os_[:, :n], ob[0:65, :n], AF.Copy, [obt], [ost])
                    recip(os_[64:65, :n], os_[64:65, :n], [ost], [ost])

                def st_b():
                    bc = e_pool.next()
                    mm(bc[0][0:64, :n], ones_f[0:65, 0:64], os_[0:65, :n], True, True, [ost, const_t], [bc[1]])
                    tt(at_[:, :n], bc[0][0:64, :n], os_[0:64, :n], ALU.mult, [bc[1], ost], [att_])

                def st_c(m):
                    def f():
                        bk, bt = o_pool.next()
                        mm(bk[:, :n], wo_h[:, m * P:(m + 1) * P], at_[:, :n], True, True, [wo_ht, att_], [bt])
                        stt(xT[:, m, t0:t0 + n], bk[:, :n], drv_ap(l, 1, 1, m, var), xT[:, m, t0:t0 + n], ALU.mult, ALU.add,
                            [bt, drv_t, x_t[m][bi]], [x_t[m][bi]])
                    return f
                return [st_a, st_b] + [st_c(m) for m in range(KC)]

            pending = []
            TAIL_AT = {2, 5, 8, 10, 12, 14, 16, 18, 20, 22}

            def expand_kv(h):
                grp = 0
                for kb in range(17):
                    k0 = kb * 512
                    kn = min(512, NKEY - k0)
                    bk, bt = kv_pool.next()
                    if kb == 0:
                        src_t = [ckva_t[0], ckva_t[1]]
                    else:
                        src_t = [ckva_t[1 + (k0 - C_TOK) // L_TOK], ckva_t[1 + min(3, (k0 + kn - 1 - C_TOK) // L_TOK)]]
                    mm(bk[0:64, :kn], wuk[:, h * 64:(h + 1) * 64], ckv_all[:, k0:k0 + kn], True, True, [slot_t["C"], sA] + src_t, [bt])
                    evac_copy(Kt[0:64, k0:k0 + kn], bk[0:64, :kn], [bt], [K_t[kb]])
                    grp += 1
                    if grp % 2 == 0 and len(pending) > 10:
                        pending.pop(0)()
                for kg in range(9):
                    c0 = kg * 8
                    cn = min(8, NKC - c0)
                    bk, bt = kv_pool.next()
                    for i in range(cn):
                        kc = c0 + i
                        mm(bk[:, i * 64:(i + 1) * 64], ckv_all[:, kc * P:(kc + 1) * P], wuv[:, h * 64:(h + 1) * 64], True, True,
                           [slot_t["C"], sA] + ckva_t, [bt])
                    evac_copy(Vt[:, c0:c0 + cn, 0:64], bk[:, 0:cn * 64].rearrange("p (a b) -> p a b", b=64), [bt], [V_t[kg]])
                    grp += 1
                    if grp % 2 == 0 and len(pending) > 10:
                        pending.pop(0)()
                wo_h, wo_ht = woh[h % 2], woh_t[h % 2]
                dma("pool", [(wo_h[:, :], wout_d[l, h * 64:(h + 1) * 64, :])], [], [wo_ht], "d_woh%d" % (h % 2))

            next_q = None
            pend = []

            def pv(kc, pb, pbt, first, lastc, ob, obt, n):
                mm(ob[0:65, :n], Vt[:, kc, 0:65], pb[:, :n], first, lastc, [V_t[kc // 8], Vones_t, pbt], [obt])
            for it, (h, bi) in enumerate(items):
                t0, n, var = BLOCKS[bi]
                if bi == qblocks[0]:
                    while pend:
                        pv(*pend.pop(0))
                    expand_kv(h)
                while len(pending) > 10:
                    pending.pop(0)()
                if next_q is None:
                    qst, qb_, qbt = q_stages(h, bi)
                    for f in qst:
                        f()
                else:
                    qb_, qbt = next_q
                next_q = None
                nq_stages = []
                chunks = list(range(NKC)) if var == 0 else [0, 1]
                ob, obt = oacc_pool.next()
                for ci, kc in enumerate(chunks):
                    sb_, sbt = s_pool.next()
                    mm(sb_[:, :n], Kt[0:96, kc * P:(kc + 1) * P], qb_[0:96, :n], True, True, [K_t[kc // 4], Kpe_t[0 if kc < 2 else 1 + (kc - 2) // 16], qbt], [sbt])
                    pb, pbt = Pt[cnt_["p"] % 3], P_t[cnt_["p"] % 3]
                    cnt_["p"] += 1
                    act(pb[:, :n], sb_[:, :n], AF.Exp, [sbt], [pbt], scale=ATTN_SCALE)
                    pend.append((kc, pb, pbt, ci == 0, ci == len(chunks) - 1, ob, obt, n))
                    if len(pend) > 2:
                        pv(*pend.pop(0))
                    if ci in TAIL_AT and pending:
                        pending.pop(0)()
                    if ci == 30 and it + 1 < len(items):
                        nq_stages, nqb, nqt = q_stages(*items[it + 1])
                        next_q = (nqb, nqt)
                        nq_stages.pop(0)()
                    if ci == 36 and nq_stages:
                        nq_stages.pop(0)()
                for f in nq_stages:
                    f()
                pending.extend(tail_stages(h, bi, ob, obt))
            while pend:
                pv(*pend.pop(0))
            while pending:
                pending.pop(0)()
            S.barrier()
            return False

        def final(with_norm=True):
            AR.reset()
            fnw = AR.alloc([D], F32); fnw_t = T("fnw")
            ost = [AR.alloc([D], F32) for _ in range(2)]; ost_t = [T("ost0"), T("ost1")]
            junk = AR.alloc([512], F32); junk_t = T("junk")
            ssq = AR.alloc([16, 4], F32); ssq_t = [T("ssq%d" % i) for i in range(16)]
            dma("sp", [(fnw[:, :], fnw_d[:, :])], [], [fnw_t], "d_fnw")
            tp = bankpool([0, 1, 2, 3, 4, 5])
            for tk in range(L_TOK // P):
                bi = tk // 4
                hb = [tp.next(), tp.next()]
                for c in range(KC):
                    bk, bt = hb[c // 4]
                    tr(bk[:, (c % 4) * P:(c % 4 + 1) * P], xT[:, c, tk * P:(tk + 1) * P], [x_t[c][bi], const_t], [bt])
                sq_ = ssq[:, tk, :]
                sqt = ssq_t[tk]
                o_, ot = ost[tk % 2], ost_t[tk % 2]
                if with_norm:
                    for hf in range(2):
                        act(junk[:, :], hb[hf][0][:, :], AF.Square, [hb[hf][1], sqt], [junk_t, sqt], accum_out=sq_[:, hf:hf + 1])
                    tt(sq_[:, 2:3], sq_[:, 0:1], sq_[:, 1:2], ALU.add, [sqt], [sqt])
                    act(sq_[:, 3:4], sq_[:, 2:3], AF.Sqrt, [sqt, par_t], [sqt], bias=eps_ap, scale=1.0 / D)
                    recip(sq_[:, 2:3], sq_[:, 3:4], [sqt], [sqt])
                    for hf in range(2):
                        stt(o_[:, hf * 512:(hf + 1) * 512], hb[hf][0][:, :], sq_[:, 2:3], fnw[:, hf * 512:(hf + 1) * 512], ALU.mult, ALU.mult,
                            [hb[hf][1], sqt, fnw_t, ot], [ot])
                else:
                    for hf in range(2):
                        evac_copy(o_[:, hf * 512:(hf + 1) * 512], hb[hf][0][:, :], [hb[hf][1], ot], [ot])
                dma("sp", [(out_d[tk * P:(tk + 1) * P, :], o_[:, :])], [ot], [out_tiles[tk]], "d_out%d" % (tk % 2))

        S.barrier()
        stopped = False
        if stop_after != "prologue":
            for l in range(DEPTH):
                last = (l == DEPTH - 1)
                ffn(l, 0, [0, 1, 2, 3, 4])
                if stop_after == ("ffn1", l):
                    stopped = True
                    break
                if mixer(l):
                    stopped = True
                    break
                if stop_after == ("mixer", l):
                    stopped = True
                    break
                ffn(l, 1, [0, 1, 2, 3] if last else [0, 1, 2, 3, 4], first_slot="B")
                if stop_after == ("ffn2", l):
                    stopped = True
                    break
        else:
            stopped = True
        final(with_norm=not stopped)
        S.barrier(engines=("sp",))
        cnt = S.finalize()
        S.check_deadlock()
        sems = {}
        for ch in cnt:
            sems[ch] = es.enter_context(nc.semaphore("s_" + ch))
        block = es.enter_context(nc.Block())

        @block.tensor
        def _(eng):
            S.emit("pe", eng, sems)

        @block.scalar
        def _(eng):
            S.emit("act", eng, sems)

        @block.vector
        def _(eng):
            S.emit("dve", eng, sems)

        @block.sync
        def _(eng):
            S.emit("sp", eng, sems)

        @block.gpsimd
        def _(eng):
            S.emit("pool", eng, sems)
    return nc, AR.hi, cnt, {e: len(S.ops[e]) for e in ENGS}


def _fm(v):
    return np.ascontiguousarray(np.asarray(v, np.float32).reshape(-1, P).T)


def _rope_tables(q):
    t = np.arange(q * L_TOK, (q + 1) * L_TOK)
    row = (t // 64).astype(np.float32)
    col = (t % 64).astype(np.float32)
    d_axis = 16
    inv = (np.float32(10000.0) ** (-np.arange(0, d_axis, 2, dtype=np.float32) / np.float32(d_axis))).astype(np.float32)
    ar = row[:, None] * inv
    ac = col[:, None] * inv
    cr, sr, cc, sc = np.cos(ar), np.sin(ar), np.cos(ac), np.sin(ac)
    C = np.concatenate([cr, cr, cc, cc], axis=1).T
    Sg = np.concatenate([-sr, sr, -sc, sc], axis=1).T
    return np.ascontiguousarray(np.stack([C, Sg], axis=1).astype(np.float32)).astype(ml_dtypes.bfloat16)


_SW = np.concatenate([np.arange(8, 16), np.arange(0, 8), np.arange(24, 32), np.arange(16, 24)])


def _prep_shared(w_ada, b_ada, norm_w, w_ffn1_in, w_ffn1_out, w_ffn2_in, w_ffn2_out, w_in, q_norm_w, kv_norm_w,
                 w_uq, w_ukv, conv_w, w_out, final_norm_w):
    f = lambda a: np.ascontiguousarray(np.asarray(a, np.float32))
    w_in = f(w_in)
    cq, ckv, kpe = w_in[:, :, 0:256], w_in[:, :, 256:384], w_in[:, :, 384:416]
    gb, gc, xv = w_in[:, :, 416:928], w_in[:, :, 928:1440], w_in[:, :, 1440:1952]
    kpe_sw = kpe[:, :, _SW]
    w_in_r = np.concatenate([cq, ckv, gb, gc, xv, kpe, kpe_sw, kpe, kpe_sw], axis=2)
    w_uq = f(w_uq).reshape(DEPTH, 256, NH, 96)
    pe_sw = w_uq[:, :, :, 64:96][:, :, :, _SW]
    w_uq_r = np.concatenate([w_uq, w_uq[:, :, :, 0:64], pe_sw], axis=3).reshape(DEPTH, 256, NH * 192)
    w_ukv = f(w_ukv).reshape(DEPTH, 128, NH, 128)
    w_uk_r = np.ascontiguousarray(w_ukv[:, :, :, 0:64].reshape(DEPTH, 128, 512))
    w_uv_r = np.ascontiguousarray(w_ukv[:, :, :, 64:128].reshape(DEPTH, 128, 512))
    shared = {
        "w_ada": f(w_ada), "w_ffn1_in": f(w_ffn1_in), "w_ffn1_out": f(w_ffn1_out),
        "w_ffn2_in": f(w_ffn2_in), "w_ffn2_out": f(w_ffn2_out),
        "w_in_r": np.ascontiguousarray(w_in_r), "w_uq_r": np.ascontiguousarray(w_uq_r),
        "w_uk_r": w_uk_r, "w_uv_r": w_uv_r, "w_out": f(w_out),
        "fnw_bc": np.ascontiguousarray(np.broadcast_to(f(final_norm_w)[None, :], (P, D))),
        "ident": np.eye(P, dtype=np.float32),
    }
    par = np.zeros((P, NPAR), np.float32)
    for l in range(DEPTH):
        par[:, PC_BADA + l * 72:PC_BADA + (l + 1) * 72] = _fm(f(b_ada)[l])
        for s in range(3):
            par[:, PC_NW + (l * 3 + s) * 8:PC_NW + (l * 3 + s) * 8 + 8] = _fm(f(norm_w)[l, s])
        par[:, PC_QNW + l * 2:PC_QNW + l * 2 + 2] = _fm(f(q_norm_w)[l])
        par[:, PC_KVNW + l:PC_KVNW + l + 1] = _fm(f(kv_norm_w)[l])
        for k in range(3):
            par[:, PC_CONVW + (l * 3 + k) * 4:PC_CONVW + (l * 3 + k) * 4 + 4] = _fm(f(conv_w)[l, k])
    par[:, PC_EPS] = EPS
    return shared, par


_NC_CACHE = {}
_STOP_AFTER = None


def _in_maps(x, c, ctx, c_ctx, w_ada, b_ada, norm_w, w_ffn1_in, w_ffn1_out, w_ffn2_in, w_ffn2_out,
             w_in, q_norm_w, kv_norm_w, w_uq, w_ukv, conv_w, w_out, final_norm_w):
    x = np.asarray(x, np.float32)
    ctx = np.asarray(ctx, np.float32)
    c = np.asarray(c, np.float32)
    c_ctx = np.asarray(c_ctx, np.float32)
    shared, par0 = _prep_shared(w_ada, b_ada, norm_w, w_ffn1_in, w_ffn1_out, w_ffn2_in, w_ffn2_out, w_in,
                                q_norm_w, kv_norm_w, w_uq, w_ukv, conv_w, w_out, final_norm_w)
    in_maps = []
    for r in range(8):
        b, q = r // 4, r % 4
        par = par0.copy()
        cv = np.stack([_fm(c[b]), _fm(c_ctx)], axis=2)
        par[:, PC_CVEC:PC_CVEC + 16] = cv.reshape(P, 16)
        mask = np.zeros((4, 8), np.float32)
        for cc in range(4):
            if q - 1 >= 0:
                mask[q - 1, cc * 2 + 1] = 1.0
            if q + 1 < 4:
                mask[q + 1, cc * 2 + 0] = 1.0
        par[:, PC_MASK:PC_MASK + 32] = mask.reshape(1, 32)
        m = dict(shared)
        m["x"] = np.ascontiguousarray(x[b, q * L_TOK:(q + 1) * L_TOK, :])
        m["ctx"] = np.ascontiguousarray(ctx[b])
        m["par"] = par
        m["rope"] = _rope_tables(q)
        in_maps.append(m)
    return in_maps


def kernel(**inputs):
    in_maps = _in_maps(**inputs)
    if "nc" not in _NC_CACHE:
        _NC_CACHE["nc"] = build(_STOP_AFTER)[0]
    nc = _NC_CACHE["nc"]
    res = run_bass_kernel_spmd(nc, in_maps, core_ids=list(range(8)))
    out = np.empty((2, 8192, D), np.float32)
    for r in range(8):
        b, q = r // 4, r % 4
        out[b, q * L_TOK:(q + 1) * L_TOK, :] = res.results[r]["out"]
    return out
```

```python
import math
import contextlib
import numpy as np
import ml_dtypes
import concourse.bass as bass
import concourse.mybir as mybir
from concourse.bass_utils import run_bass_kernel_spmd

F32 = mybir.dt.float32
BF16 = mybir.dt.bfloat16
AF = mybir.ActivationFunctionType
ALU = mybir.AluOpType

P = 128
D = 1024
KC = 8
DFF = 2816
FC = 22
DEPTH = 2
NH = 8
L_TOK = 2048
C_TOK = 256
NTOK = L_TOK + C_TOK
NKEY = 8192 + C_TOK
NKC = NKEY // P
EPS = 1e-6
ATTN_SCALE = 1.0 / math.sqrt(96.0)
PAYW = L_TOK + 8
GROUPS = [(0, 4), (4, 4), (8, 4), (12, 4), (16, 3), (19, 3)]

PC_CVEC = 0
PC_BADA = 16
PC_NW = 160
PC_QNW = 208
PC_KVNW = 212
PC_CONVW = 214
PC_MASK = 238
PC_EPS = 270
NPAR = 272

BLOCKS = [(0, 512, 0), (512, 512, 0), (1024, 512, 0), (1536, 512, 0), (2048, 256, 1)]


class T:
    __slots__ = ("name", "w", "r")

    def __init__(self, name=""):
        self.name = name
        self.w = None
        self.r = {}


class Op:
    __slots__ = ("eng", "fn", "deps", "sig", "chan", "val", "ninc", "kind")


COMPUTE = ("pe", "act", "dve")
ENGS = ("pe", "act", "dve", "sp", "pool")


class Sched:
    def __init__(self):
        self.ops = {e: [] for e in ENGS}
        self.chan_last = {}
        self.chan_eng = {}
        self.chan_kind = {}

    def add(self, eng, name, kwargs, reads=(), writes=(), chan=None, ninc=1, kind=None):
        op = Op()
        op.eng = eng
        op.fn = (name, kwargs)
        op.sig = False
        op.val = 0
        op.ninc = ninc
        if chan is None:
            chan = eng
            kind = "c"
        elif kind is None:
            kind = "d"
        op.chan = chan
        op.kind = kind
        assert self.chan_eng.setdefault(chan, eng) == eng
        self.chan_kind[chan] = kind
        deps = set()
        for t in reads:
            if t.w is not None:
                deps.add(t.w)
        for t in writes:
            if t.w is not None:
                deps.add(t.w)
            deps.update(t.r.values())
        if eng == "pe":
            deps = {d for d in deps if d.eng != "pe"}
        op.deps = deps
        for t in reads:
            t.r[chan] = op
        for t in writes:
            t.w = op
            t.r = {}
        self.ops[eng].append(op)
        self.chan_last[chan] = op
        return op

    def barrier(self, engines=("pe", "act", "dve", "sp")):
        lasts = [op for ch, op in self.chan_last.items() if self.chan_eng[ch] in engines]
        for e in engines:
            op = Op()
            op.eng = e
            op.fn = None
            op.sig = False
            op.val = 0
            op.ninc = 0
            op.chan = e if e in COMPUTE else None
            op.kind = "c"
            op.deps = set(d for d in lasts if not (d.eng == e and e == "pe"))
            self.ops[e].append(op)

    def finalize(self):
        for e in ENGS:
            for op in self.ops[e]:
                for d in op.deps:
                    d.sig = True
                if op.kind == "d" and op.fn is not None:
                    op.sig = True
        cnt = {}
        for e in ENGS:
            for op in self.ops[e]:
                if op.sig and op.fn is not None:
                    inc = 16 * op.ninc if op.kind == "d" else 1
                    cnt[op.chan] = cnt.get(op.chan, 0) + inc
                    op.val = cnt[op.chan]
        return cnt

    def check_deadlock(self):
        done = set()
        pos = {e: 0 for e in ENGS}
        progress = True
        while progress:
            progress = False
            for e in ENGS:
                ops = self.ops[e]
                while pos[e] < len(ops):
                    op = ops[pos[e]]
                    if all((id(d) in done) for d in op.deps):
                        done.add(id(op))
                        pos[e] += 1
                        progress = True
                    else:
                        break
        stuck = {e: (pos[e], len(self.ops[e])) for e in ENGS if pos[e] < len(self.ops[e])}
        if stuck:
            msg = []
            for e, (p, n) in stuck.items():
                op = self.ops[e][p]
                msg.append((e, p, n, op.fn[0] if op.fn else None, [(d.eng, d.chan, d.fn[0] if d.fn else None) for d in op.deps if id(d) not in done]))
            raise RuntimeError("deadlock in schedule: %r" % (msg,))

    def emit(self, eng, handle, sems):
        seen = {}
        for op in self.ops[eng]:
            need = {}
            for d in op.deps:
                if d.fn is None:
                    continue
                if need.get(d.chan, 0) < d.val:
                    need[d.chan] = d.val
            for ch, v in need.items():
                if seen.get(ch, 0) < v:
                    handle.wait_ge(sems[ch], v)
                    seen[ch] = v
            if op.fn is None:
                continue
            name, kw = op.fn
            if name == "dma":
                ins = [handle.dma_start(**d) for d in kw]
            else:
                ins = getattr(handle, name)(**kw)
            if op.sig:
                if op.kind == "d":
                    lst = ins if isinstance(ins, (list, tuple)) else [ins]
                    assert len(lst) == op.ninc, (len(lst), op.ninc)
                    for i in lst:
                        i.then_inc(sems[op.chan], 16)
                else:
                    ins.then_inc(sems[op.chan], 1)


def _prod(s):
    r = 1
    for v in s:
        r *= v
    return r


class Arena:
    def __init__(self, tensor, nbytes):
        self.t = tensor
        self.nbytes = nbytes
        self.off = 0
        self.hi = 0

    def reset(self, off=0):
        self.off = off

    def alloc(self, shape, dt, parts=(0, P), at=None):
        es = 4 if dt == F32 else 2
        nel = _prod(shape)
        nb = (nel * es + 31) // 32 * 32
        off = self.off if at is None else at
        if at is None:
            self.off += nb
        assert off + nb <= self.nbytes, ("arena overflow", off, nb, self.nbytes)
        self.hi = max(self.hi, off + nb)
        a = self.t[parts[0]:parts[1], off // 4: (off + nb) // 4]
        if dt != F32:
            a = a.bitcast(dt)
        a = a[:, 0:nel]
        if len(shape) == 2:
            a = a.rearrange("p (a b) -> p a b", a=shape[0])
        elif len(shape) == 3:
            a = a.rearrange("p (a b c) -> p a b c", a=shape[0], b=shape[1])
        return a


def build(stop_after=None):
    nc = bass.Bass("TRN2", target_bir_lowering=False)

    def din(name, shape, dt=F32):
        return nc.dram_tensor(name, list(shape), dt, kind="ExternalInput").ap()

    x_d = din("x", [L_TOK, D])
    ctx_d = din("ctx", [C_TOK, D])
    par_d = din("par", [P, NPAR])
    wada_d = din("w_ada", [DEPTH, D, 9 * D])
    wfi_d = [din("w_ffn1_in", [DEPTH, D, 2 * DFF]), din("w_ffn2_in", [DEPTH, D, 2 * DFF])]
    wfo_d = [din("w_ffn1_out", [DEPTH, DFF, D]), din("w_ffn2_out", [DEPTH, DFF, D])]
    win_d = din("w_in_r", [DEPTH, D, 2048])
    wuq_d = din("w_uq_r", [DEPTH, 256, 1536])
    wuk_d = din("w_uk_r", [DEPTH, 128, 512])
    wuv_d = din("w_uv_r", [DEPTH, 128, 512])
    wout_d = din("w_out", [DEPTH, D, D])
    fnw_d = din("fnw_bc", [P, D])
    rope_d = din("rope", [32, 2, L_TOK], BF16)
    ident_d = din("ident", [P, P])
    out_d = nc.dram_tensor("out", [L_TOK, D], F32, kind="ExternalOutput").ap()
    pay_d = [nc.dram_tensor("pay%d" % l, [160, PAYW], BF16) for l in range(DEPTH)]
    gath_d = [nc.dram_tensor("gath%d" % l, [4 * 160, PAYW], BF16) for l in range(DEPTH)]

    S = Sched()
    es = contextlib.ExitStack()
    with es:
        def sb(name, shape, dt):
            return es.enter_context(nc.sbuf_tensor("sb_" + name, list(shape), dt))

        xT = sb("xT", [P, KC, NTOK], F32)
        slotA = sb("slotA", [P, 12288], BF16)
        slotB = sb("slotB", [P, 12288], BF16)
        slotC = sb("slotC", [P, 4096], BF16)
        par = sb("par", [P, NPAR], F32)
        modT = sb("modT", [P, DEPTH, 72, 2], F32)
        drv = sb("drv", [P, DEPTH * 3 * 2 * KC * 2], F32)
        ident = sb("ident", [P, P], F32)
        ones_bf = sb("ones_bf", [P, P], BF16)
        ones_f = sb("ones_f", [P, 64], F32)
        scT = sb("scT", [P, KC, 2], BF16)
        ARENA_BYTES = 77696
        arena_t = sb("arena", [P, ARENA_BYTES // 4], F32)
        AR = Arena(arena_t, ARENA_BYTES)
        banks = [es.enter_context(nc.psum_tensor("bank%d" % i, [P, 512], F32)) for i in range(8)]
        bank_t = [T("bank%d" % i) for i in range(8)]

        def mm(out, lhsT, rhs, start, stop, reads, writes):
            S.add("pe", "matmul", dict(out=out, lhsT=lhsT, rhs=rhs, start=start, stop=stop), reads, writes)

        def tr(out, in_, reads, writes):
            S.add("pe", "transpose", dict(out=out, in_=in_, identity=ident[:, :]), reads, writes)

        def act(out, in_, func, reads, writes, **kw):
            S.add("act", "activation", dict(out=out, in_=in_, func=func, **kw), reads, writes)

        def tt(out, in0, in1, op, reads, writes):
            S.add("dve", "tensor_tensor", dict(out=out, in0=in0, in1=in1, op=op), reads, writes)

        def stt(out, in0, scalar, in1, op0, op1, reads, writes):
            S.add("dve", "scalar_tensor_tensor", dict(out=out, in0=in0, scalar=scalar, in1=in1, op0=op0, op1=op1), reads, writes)

        def ts(out, in0, scalar1, op0, reads, writes):
            S.add("dve", "tensor_scalar", dict(out=out, in0=in0, scalar1=scalar1, scalar2=None, op0=op0), reads, writes)

        def cp(out, in_, reads, writes):
            S.add("dve", "tensor_copy", dict(out=out, in_=in_), reads, writes)

        def recip(out, in_, reads, writes):
            S.add("dve", "reciprocal", dict(out=out, in_=in_), reads, writes)

        def memset(ap, v, writes):
            S.add("dve", "memset", dict(ap=ap, constant=v), [], writes)

        def dma(eng, pairs, reads, writes, chan):
            S.add(eng, "dma", [dict(out=o, in_=i) for (o, i) in pairs], reads, writes, chan=chan, ninc=len(pairs))

        cp_i = [0]

        def evac_copy(out_ap, in_ap, reads, writes):
            cp_i[0] += 1
            if cp_i[0] % 2:
                act(out_ap, in_ap, AF.Copy, reads, writes)
            else:
                cp(out_ap, in_ap, reads, writes)

        def drv_ap(l, s, which, c, var):
            i = ((((l * 3 + s) * 2 + which) * KC) + c) * 2 + var
            return drv[:, i:i + 1]

        def drv_vec(l, s, which, var):
            i0 = (((l * 3 + s) * 2 + which) * KC) * 2 + var
            return drv[:, i0:i0 + 2 * KC - 1:2]

        x_t = [[T("x%d_%d" % (c, b)) for b in range(5)] for c in range(KC)]
        slot_t = {"A": T("slotA"), "B": T("slotB"), "C": T("slotC")}
        par_t = T("par")
        mod_t = T("mod")
        drv_t = T("drv")
        const_t = T("const")
        sc_t = T("scT")
        out_tiles = [T("out%d" % i) for i in range(16)]

        class Rot:
            def __init__(self, items):
                self.items = items
                self.i = 0

            def next(self):
                it = self.items[self.i % len(self.items)]
                self.i += 1
                return it

        def bankpool(ids):
            return Rot([(banks[i], bank_t[i]) for i in ids])

        dma("sp", [(par[:, :], par_d[:, :])], [], [par_t], "d_par")
        dma("sp", [(ident[:, :], ident_d[:, :])], [], [const_t], "d_const")
        memset(ones_bf[:, :], 1.0, [const_t])
        memset(ones_f[:, :], 0.0, [const_t])
        memset(ones_f[64:65, :], 1.0, [const_t])

        AR.reset()
        xs = [AR.alloc([D], F32) for _ in range(3)]
        xs_t = [T("xs%d" % i) for i in range(3)]
        pp = bankpool([0, 1, 2, 3])
        for tk in range(NTOK // P):
            st = xs[tk % 3]
            stt_ = xs_t[tk % 3]
            if tk < 16:
                src = x_d[tk * P:(tk + 1) * P, :]
            else:
                src = ctx_d[(tk - 16) * P:(tk - 15) * P, :]
            dma("sp", [(st[:, :], src)], [], [stt_], "d_xs%d" % (tk % 3))
            bi = min(tk // 4, 4)
            for hf in range(2):
                bk, bt = pp.next()
                for c4 in range(4):
                    c = hf * 4 + c4
                    tr(bk[:, c4 * P:(c4 + 1) * P], st[:, c * P:(c + 1) * P], [stt_, const_t], [bt])
                evac_copy(xT[:, hf * 4:hf * 4 + 4, tk * P:(tk + 1) * P], bk[:, :].rearrange("p (a b) -> p a b", a=4),
                          [bt], [x_t[hf * 4 + c4][bi] for c4 in range(4)])

        act(scT[:, :, :], par[:, PC_CVEC:PC_CVEC + 16].rearrange("p (k n) -> p k n", n=2), AF.Silu, [par_t], [sc_t])
        slots = {"A": slotA, "B": slotB}
        slot_seq = ["A"]

        def next_slot():
            sn = slot_seq[0]
            slot_seq[0] = "B" if sn == "A" else "A"
            return sn, slots[sn]

        for l in range(DEPTH):
            bk, bt = banks[7], bank_t[7]
            for g in range(6):
                sn, sl = next_slot()
                wv = sl[:, 0:KC * 1536].rearrange("p (k n) -> p k n", k=KC)
                dma("pool", [(wv, wada_d[l, :, g * 1536:(g + 1) * 1536].rearrange("(k p) n -> p k n", p=P))], [], [slot_t[sn]], "d_slot" + sn)
                for jl in range(12):
                    j = g * 12 + jl
                    for k in range(KC):
                        mm(bk[:, j * 2:j * 2 + 2], wv[:, k, jl * P:(jl + 1) * P], scT[:, k, :], k == 0, k == KC - 1, [slot_t[sn], sc_t], [bt])
            for n in range(2):
                tt(modT[:, l, :, n], bk[:, n:143 + n:2], par[:, PC_BADA + l * 72:PC_BADA + (l + 1) * 72], ALU.add, [bt, par_t], [mod_t])
            for s in range(3):
                for n in range(2):
                    nw = par[:, PC_NW + (l * 3 + s) * 8:PC_NW + (l * 3 + s) * 8 + 8]
                    stt(drv_vec(l, s, 0, n), modT[:, l, (3 * s + 1) * 8:(3 * s + 1) * 8 + 8, n], 1.0, nw, ALU.add, ALU.mult, [mod_t, par_t], [drv_t])
                    gsc = 1.0 if s == 1 else 0.5
                    ts(drv_vec(l, s, 1, n), modT[:, l, (3 * s + 2) * 8:(3 * s + 2) * 8 + 8, n], gsc, ALU.mult, [mod_t], [drv_t])

        def shift_ap(l, s, c, var):
            return modT[:, l, 3 * s * 8 + c, var:var + 1]

        eps_ap = par[:, PC_EPS:PC_EPS + 1]

        def norm_mod(l, s, bi, h_aps, h_tiles, tmp):
            t0, n, var = BLOCKS[bi]
            sqr, sq_t, s_ap, s_t, r_ap, r_t, tt_, tt_t, ss_pool = tmp
            bk, bt = ss_pool.next()
            for c in range(KC):
                q, qt = sqr[c % len(sqr)], sq_t[c % len(sqr)]
                act(q[:, :n], xT[:, c, t0:t0 + n], AF.Square, [x_t[c][bi]], [qt])
                mm(bk[:, :n], ones_bf[:, :], q[:, :n], c == 0, c == KC - 1, [qt, const_t], [bt])
            act(s_ap[:, :n], bk[:, :n], AF.Sqrt, [bt, par_t], [s_t], bias=eps_ap, scale=1.0 / D)
            recip(r_ap[:, :n], s_ap[:, :n], [s_t], [r_t])
            for c in range(KC):
                tb, tbt = tt_[c % 2], tt_t[c % 2]
                stt(tb[:, :n], xT[:, c, t0:t0 + n], drv_ap(l, s, 0, c, var), r_ap[:, :n], ALU.mult, ALU.mult, [x_t[c][bi], drv_t, r_t], [tbt])
                act(h_aps[c], tb[:, :n], AF.Identity, [tbt, mod_t], [h_tiles[c]], bias=shift_ap(l, s, c, var), scale=1.0)

        def ffn(l, f, bis, first_slot=None):
            s = 0 if f == 0 else 2
            if first_slot is not None:
                slot_seq[0] = first_slot
            AR.reset()
            hT = AR.alloc([KC, NTOK], BF16)
            h_t = [[T("h%d_%d" % (c, b)) for b in range(5)] for c in range(KC)]
            actb = [AR.alloc([4, 512], BF16) for _ in range(2)]
            act_t = [[T("act%d_%d" % (i, j)) for j in range(4)] for i in range(2)]
            sqr = [AR.alloc([512], BF16) for _ in range(3)]
            sq_t = [T("sq%d" % i) for i in range(3)]
            s_ap = AR.alloc([512], F32); s_t = T("s")
            r_ap = AR.alloc([512], F32); r_t = T("r")
            tt_ = [AR.alloc([512], F32) for _ in range(2)]
            tt_t = [T("tt0"), T("tt1")]
            sg = [AR.alloc([512], F32) for _ in range(2)]
            sg_t = [T("sg0"), T("sg1")]
            tmp = (sqr, sq_t, s_ap, s_t, r_ap, r_t, tt_, tt_t, bankpool([6, 7]))
            def do_norm(bi):
                t0, n, var = BLOCKS[bi]
                norm_mod(l, s, bi, [hT[:, c, t0:t0 + n] for c in range(KC)], [h_t[c][bi] for c in range(KC)], tmp)
            do_norm(bis[0])
            gu_pool = bankpool([0, 1, 2, 3])
            o_pool = bankpool([4, 5])
            ai = 0
            sgi = 0
            for (j0, ng) in GROUPS:
                sn, sl = next_slot()
                wg = sl[:, 0:KC * 512].rearrange("p (k n) -> p k n", k=KC)
                wu = sl[:, 4096:4096 + KC * 512].rearrange("p (k n) -> p k n", k=KC)
                wo = sl[:, 8192:8192 + 4 * D].rearrange("p (j n) -> p j n", j=4)
                st = slot_t[sn]
                dma("pool", [(wg[:, :, 0:ng * P], wfi_d[f][l, :, j0 * P:(j0 + ng) * P].rearrange("(k p) n -> p k n", p=P)),
                             (wu[:, :, 0:ng * P], wfi_d[f][l, :, DFF + j0 * P:DFF + (j0 + ng) * P].rearrange("(k p) n -> p k n", p=P)),
                             (wo[:, 0:ng, :], wfo_d[f][l, j0 * P:(j0 + ng) * P, :].rearrange("(j p) n -> p j n", p=P))],
                    [], [st], "d_slot" + sn)

                def emit_out(bi, ab, abt):
                    t0, n, var = BLOCKS[bi]
                    for m in range(KC):
                        bk, bt = o_pool.next()
                        for jl in range(ng):
                            mm(bk[:, :n], wo[:, jl, m * P:(m + 1) * P], ab[:, jl, :n], jl == 0, jl == ng - 1, [st, abt[jl]], [bt])
                        stt(xT[:, m, t0:t0 + n], bk[:, :n], drv_ap(l, s, 1, m, var), xT[:, m, t0:t0 + n], ALU.mult, ALU.add,
                            [bt, drv_t, x_t[m][bi]], [x_t[m][bi]])
                pending = None
                for bi in bis:
                    t0, n, var = BLOCKS[bi]
                    ab, abt = actb[ai % 2], act_t[ai % 2]
                    ai += 1
                    for jl in range(ng):
                        gb_, gbt = gu_pool.next()
                        ub_, ubt = gu_pool.next()
                        for k in range(KC):
                            mm(gb_[:, :n], wg[:, k, jl * P:(jl + 1) * P], hT[:, k, t0:t0 + n], k == 0, k == KC - 1, [st, h_t[k][bi]], [gbt])
                        for k in range(KC):
                            mm(ub_[:, :n], wu[:, k, jl * P:(jl + 1) * P], hT[:, k, t0:t0 + n], k == 0, k == KC - 1, [st, h_t[k][bi]], [ubt])
                        sgb, sgt = sg[sgi % 2], sg_t[sgi % 2]
                        sgi += 1
                        act(sgb[:, :n], gb_[:, :n], AF.Silu, [gbt], [sgt])
                        tt(ab[:, jl, :n], ub_[:, :n], sgb[:, :n], ALU.mult, [ubt, sgt], [abt[jl]])
                        if j0 == 0 and jl == 0 and bis.index(bi) + 1 < len(bis):
                            do_norm(bis[bis.index(bi) + 1])
                    if pending is not None:
                        emit_out(*pending)
                    pending = (bi, ab, abt)
                emit_out(*pending)
            S.barrier()

        def mixer(l):
            last = (l == DEPTH - 1)
            AR.reset()
            cqn = AR.alloc([2, NTOK], BF16); cqn_t = [[T("cqn%d_%d" % (c, b)) for b in range(5)] for c in range(2)]
            tab_off = AR.off
            tab = AR.alloc([2, L_TOK], BF16, parts=(64, 96)); tab_t = T("tab")
            ckv_ctx = AR.alloc([C_TOK], BF16); ckvc_t = T("ckvctx")
            kpe_ctx = AR.alloc([C_TOK], BF16, parts=(64, 96)); kpec_t = T("kpectx")
            hg = AR.alloc([4, 8], BF16); hg_t = T("hg")
            edge = AR.alloc([4, 4, 4], F32); edge_t = T("edge")
            nb = AR.alloc([4, 4, 2], F32); nb_t = T("nb")
            hsel = AR.alloc([4, 8], F32); hsel_t = T("hsel")
            hacc = AR.alloc([8], F32); hacc_t = T("hacc")
            base_off = AR.off
            conv = AR.alloc([4, NTOK], BF16); conv_end = AR.off; conv_t = [[T("conv%d_%d" % (c, b)) for b in range(5)] for c in range(4)]
            h2 = AR.alloc([KC, 512], BF16); h2_t = [T("h2_%d" % c) for c in range(KC)]
            gbb = AR.alloc([4, 512], BF16); gbb_t = [T("gbb%d" % c) for c in range(4)]
            ub = AR.alloc([4, 512], BF16); ub_t = [T("ub%d" % c) for c in range(4)]
            gcs = [AR.alloc([512], F32) for _ in range(2)]; gcs_t = [T("gcs0"), T("gcs1")]
            sqr = [AR.alloc([512], BF16) for _ in range(3)]; sq_t = [T("sq%d" % i) for i in range(3)]
            s_ap = AR.alloc([512], F32); s_t = T("s")
            r_ap = AR.alloc([512], F32); r_t = T("r")
            tt_ = [AR.alloc([512], F32) for _ in range(2)]; tt_t = [T("tt0"), T("tt1")]
            yb = tt_; yb_t = tt_t
            s2 = [s_ap, s_ap]; s2_t = [s_t, s_t]
            r2 = [r_ap, r_ap]; r2_t = [r_t, r_t]
            rt1 = AR.alloc([512], F32, parts=(64, 96)); rt1_t = T("rt1")
            rt2 = AR.alloc([512], F32, parts=(64, 96)); rt2_t = T("rt2")
            stg = [AR.alloc([512], BF16) for _ in range(2)]; stg_t = [T("stg0"), T("stg1")]
            kst = [AR.alloc([512], BF16, parts=(64, 96)) for _ in range(2)]; kst_t = [T("kst0"), T("kst1")]
            pay_tiles = []
            gath_t = T("gath")

            winA = slotA[:, 0:KC * 1536].rearrange("p (k n) -> p k n", k=KC)
            winB = slotB[:, 0:KC * 512].rearrange("p (k n) -> p k n", k=KC)
            woc = slotB[:, 4096:4096 + 4 * D].rearrange("p (j n) -> p j n", j=4)
            wuq = slotC[:, 0:2 * 1536].rearrange("p (k n) -> p k n", k=2)
            wuk = slotC[:, 3072:3584]
            wuv = slotC[:, 3584:4096]
            dma("pool", [(winA, win_d[l, :, 0:1536].rearrange("(k p) n -> p k n", p=P))], [], [slot_t["A"]], "d_slotA")
            dma("pool", [(winB, win_d[l, :, 1536:2048].rearrange("(k p) n -> p k n", p=P)),
                         (woc, wout_d[l, 512:1024, :].rearrange("(j p) n -> p j n", p=P))], [], [slot_t["B"]], "d_slotB")
            dma("pool", [(wuq, wuq_d[l, :, :].rearrange("(k p) n -> p k n", p=P)), (wuk, wuk_d[l, :, :]), (wuv, wuv_d[l, :, :])],
                [], [slot_t["C"]], "d_slotC")
            dma("sp", [(tab, rope_d[:, :, :])], [], [tab_t], "d_tab")

            def win_ap(k, c0, c1):
                if c1 <= 1536:
                    return winA[:, k, c0:c1], slot_t["A"]
                assert c0 >= 1536
                return winB[:, k, c0 - 1536:c1 - 1536], slot_t["B"]

            gen_pool = bankpool([0, 1, 2, 3])
            hold_pool = bankpool([4, 5, 6])
            ss_pool = bankpool([7])
            nm_tmp = (sqr, sq_t, s_ap, s_t, r_ap, r_t, tt_, tt_t, ss_pool)

            def cw(k, c):
                i = PC_CONVW + (l * 3 + k) * 4 + c
                return par[:, i:i + 1]

            for bi in [0, 1, 2, 3, 4]:
                t0, n, var = BLOCKS[bi]
                ctx_only_kv = (var == 1 and last)
                if bi == 0:
                    norm_mod(l, 1, 0, [h2[:, c, :n] for c in range(KC)], h2_t, nm_tmp)

                def proj(bk, bt, c0, c1, M):
                    for k in range(KC):
                        w, wt = win_ap(k, c0, c1)
                        mm(bk[0:M, :n], w, h2[:, k, :n], k == 0, k == KC - 1, [wt, h2_t[k]], [bt])

                def rms_small(pss, nfeat, normw_col, outs, out_tiles_):
                    bk2, bt2 = ss_pool.next()
                    for i, (bk, bt) in enumerate(pss):
                        q, qt = sqr[i % 3], sq_t[i % 3]
                        act(q[:, :n], bk[:, :n], AF.Square, [bt], [qt])
                        mm(bk2[:, :n], ones_bf[:, :], q[:, :n], i == 0, i == len(pss) - 1, [qt, const_t], [bt2])
                    ix = 0 if len(pss) == 2 else 1
                    act(s2[ix][:, :n], bk2[:, :n], AF.Sqrt, [bt2, par_t], [s2_t[ix]], bias=eps_ap, scale=1.0 / nfeat)
                    recip(r2[ix][:, :n], s2[ix][:, :n], [s2_t[ix]], [r2_t[ix]])
                    for i, (bk, bt) in enumerate(pss):
                        stt(outs[i], bk[:, :n], par[:, normw_col + i:normw_col + i + 1], r2[ix][:, :n], ALU.mult, ALU.mult,
                            [bt, par_t, r2_t[ix]], [out_tiles_[i]])

                kv = hold_pool.next()
                proj(kv[0], kv[1], 256, 384, P)
                if var == 0:
                    sg_, sgt_ = stg[bi % 2], stg_t[bi % 2]
                    rms_small([kv], 128, PC_KVNW + l, [sg_[:, :n]], [sgt_])
                    pt_ = T("pay")
                    pay_tiles.append(pt_)
                    dma("sp", [(pay_d[l][0:128, t0:t0 + n], sg_[:, :n])], [sgt_], [pt_], "d_pay%d" % (bi % 2))
                else:
                    rms_small([kv], 128, PC_KVNW + l, [ckv_ctx[:, :n]], [ckvc_t])
                ka = gen_pool.next()
                proj(ka[0], ka[1], 1920, 2016, 96)
                if var == 0:
                    kb_ = gen_pool.next()
                    proj(kb_[0], kb_[1], 1952, 2048, 96)
                    ks_, kst__ = kst[bi % 2], kst_t[bi % 2]
                    tt(rt1[:, :n], ka[0][64:96, :n], tab[:, 0, t0:t0 + n], ALU.mult, [ka[1], tab_t], [rt1_t])
                    tt(rt2[:, :n], kb_[0][64:96, :n], tab[:, 1, t0:t0 + n], ALU.mult, [kb_[1], tab_t], [rt2_t])
                    tt(ks_[:, :n], rt1[:, :n], rt2[:, :n], ALU.add, [rt1_t, rt2_t], [kst__])
                    pt_ = T("payk")
                    pay_tiles.append(pt_)
                    pairs = [(pay_d[l][128:160, t0:t0 + n], ks_[:, :n])]
                    if bi == 3:
                        pairs.append((pay_d[l][128:160, L_TOK:L_TOK + 8], ks_[:, 0:8]))
                    dma("sp", pairs, [kst__], [pt_], "d_payk%d" % (bi % 2))
                else:
                    act(kpe_ctx[:, :n], ka[0][64:96, :n], AF.Copy, [ka[1]], [kpec_t])
                if ctx_only_kv:
                    continue
                cq = [hold_pool.next(), hold_pool.next()]
                for i in range(2):
                    proj(cq[i][0], cq[i][1], i * P, (i + 1) * P, P)
                rms_small(cq, 256, PC_QNW + l * 2, [cqn[:, i, t0:t0 + n] for i in range(2)], [cqn_t[0][bi], cqn_t[1][bi]])
                for c in range(4):
                    g_ = gen_pool.next()
                    proj(g_[0], g_[1], 384 + c * P, 384 + (c + 1) * P, P)
                    act(gbb[:, c, :n], g_[0][:, :n], AF.Copy, [g_[1]], [gbb_t[c]])
                for c in range(4):
                    g_ = gen_pool.next()
                    proj(g_[0], g_[1], 896 + c * P, 896 + (c + 1) * P, P)
                    act(gcs[c % 2][:, :n], g_[0][:, :n], AF.Copy, [g_[1]], [gcs_t[c % 2]])
                    v_ = gen_pool.next()
                    proj(v_[0], v_[1], 1408 + c * P, 1408 + (c + 1) * P, P)
                    tt(ub[:, c, :n], v_[0][:, :n], gcs[c % 2][:, :n], ALU.mult, [v_[1], gcs_t[c % 2]], [ub_t[c]])
                if bi + 1 < 5:
                    n_nx = BLOCKS[bi + 1][1]
                    norm_mod(l, 1, bi + 1, [h2[:, c, :n_nx] for c in range(KC)], h2_t, nm_tmp)
                for c in range(4):
                    y, yt = yb[c % 2], yb_t[c % 2]
                    ts(y[:, :n], ub[:, c, :n], cw(1, c), ALU.mult, [ub_t[c], par_t], [yt])
                    stt(y[:, 1:n], ub[:, c, 0:n - 1], cw(0, c), y[:, 1:n], ALU.mult, ALU.add, [ub_t[c], par_t, yt], [yt])
                    stt(y[:, 0:n - 1], ub[:, c, 1:n], cw(2, c), y[:, 0:n - 1], ALU.mult, ALU.add, [ub_t[c], par_t, yt], [yt])
                    tt(conv[:, c, t0:t0 + n], y[:, :n], gbb[:, c, :n], ALU.mult, [yt, gbb_t[c]], [conv_t[c][bi]])
                if var == 0:
                    cp(edge[:, :, bi, 0:2], ub[:, :, 0:n:n - 1], ub_t, [edge_t])
                    cp(edge[:, :, bi, 2:4], gbb[:, :, 0:n:n - 1], gbb_t, [edge_t])
                if bi == 3:
                    hst = stg[0]
                    hstv = hst[:, 0:8].rearrange("p (c e) -> p c e", e=2)
                    cp(hstv[:, :, 0], edge[:, :, 0, 0], [edge_t], [stg_t[0]])
                    cp(hstv[:, :, 1], edge[:, :, 3, 1], [edge_t, stg_t[0]], [stg_t[0]])
                    pt_ = T("payh")
                    pay_tiles.append(pt_)
                    dma("sp", [(pay_d[l][0:128, L_TOK:L_TOK + 8], hst[:, 0:8])], [stg_t[0]], [pt_], "d_pay0")
                    if stop_after != ("inproj_nocc", l):
                      S.add("pool", "collective_compute", dict(kind="AllGather", op=ALU.bypass, replica_groups=[[0, 1, 2, 3], [4, 5, 6, 7]],
                                                              ins=[pay_d[l].ap().opt()], outs=[gath_d[l].ap().opt()]),
                          pay_tiles, [gath_t], chan="cc", kind="cc")
            S.barrier()
            if stop_after == ("cc", l):
                dma("sp", [(hg, gath_d[l].ap().rearrange("(j r) n -> r j n", r=160)[0:128, :, L_TOK:L_TOK + 8])], [gath_t], [hg_t], "d_hg")
                S.barrier()
                return True
            if stop_after in (("inproj", l), ("inproj_nocc", l)):
                return True

            AR.reset(base_off)
            Vt = AR.alloc([NKC, 65], BF16); V_t = [T("V%d" % i) for i in range(9)]; Vones_t = T("Vones")
            Pt = [AR.alloc([512], BF16) for _ in range(3)]; P_t = [T("P%d" % i) for i in range(3)]
            qt = [AR.alloc([512], BF16, parts=(0, 96)) for _ in range(2)]; q_t = [T("q0"), T("q1")]
            osb = [AR.alloc([512], F32, parts=(0, 65)) for _ in range(2)]; osb_t = [T("osb0"), T("osb1")]
            qr1 = AR.alloc([512], F32, parts=(64, 96)); qr1_t = T("qr1")
            qr2 = AR.alloc([512], F32, parts=(64, 96)); qr2_t = T("qr2")
            assert AR.off >= conv_end
            Kt = AR.alloc([NKEY], BF16, parts=(0, 96)); K_t = [T("K%d" % i) for i in range(17)]
            att = [AR.alloc([512], BF16, parts=(0, 64), at=tab_off + i * 1024) for i in range(2)]; att_t = [T("att0"), T("att1")]
            woh = [AR.alloc([D], BF16, parts=(0, 64), at=tab_off + 2048 + i * 2048) for i in range(2)]; woh_t = [T("woh0"), T("woh1")]
            ckv_all = slotA[:, 0:NKEY]
            sA = slot_t["A"]
            ckva_t = [T("ckva%d" % i) for i in range(5)]

            Kpe_t = [T("kpe%d" % i) for i in range(5)]
            for j in range(4):
                dma("sp", [(ckv_all[:, C_TOK + j * L_TOK:C_TOK + (j + 1) * L_TOK], gath_d[l][j * 160:j * 160 + 128, 0:L_TOK])],
                    [gath_t], [ckva_t[1 + j]], "d_ga%d" % j)
                dma("sp", [(Kt[64:96, C_TOK + j * L_TOK:C_TOK + (j + 1) * L_TOK], gath_d[l][j * 160 + 128:j * 160 + 160, 0:L_TOK])],
                    [gath_t], [Kpe_t[1 + j]], "d_gk%d" % j)
            dma("sp", [(hg, gath_d[l].ap().rearrange("(j r) n -> r j n", r=160)[0:128, :, L_TOK:L_TOK + 8])], [gath_t], [hg_t], "d_hg")
            cp(ckv_all[:, 0:C_TOK], ckv_ctx[:, :], [ckvc_t], [ckva_t[0]])
            act(Kt[64:96, 0:C_TOK], kpe_ctx[:, :], AF.Copy, [kpec_t], [Kpe_t[0]])

            if stop_after == ("post1", l):
                S.barrier()
                return True
            tt(hsel, hg, par[:, PC_MASK:PC_MASK + 32].rearrange("p (j n) -> p j n", j=4), ALU.mult, [hg_t, par_t], [hsel_t])
            tt(hacc[:, :], hsel[:, 0, :], hsel[:, 1, :], ALU.add, [hsel_t], [hacc_t])
            tt(hacc[:, :], hacc[:, :], hsel[:, 2, :], ALU.add, [hsel_t, hacc_t], [hacc_t])
            tt(hacc[:, :], hacc[:, :], hsel[:, 3, :], ALU.add, [hsel_t, hacc_t], [hacc_t])
            hv = hacc[:, :].rearrange("p (c e) -> p c e", e=2)
            cp(nb[:, :, 1:4, 0], edge[:, :, 0:3, 1], [edge_t], [nb_t])
            cp(nb[:, :, 0, 0], hv[:, :, 1], [hacc_t, nb_t], [nb_t])
            cp(nb[:, :, 0:3, 1], edge[:, :, 1:4, 0], [edge_t, nb_t], [nb_t])
            cp(nb[:, :, 3, 1], hv[:, :, 0], [hacc_t, nb_t], [nb_t])
            tt(nb[:, :, :, :], nb[:, :, :, :], edge[:, :, :, 2:4], ALU.mult, [edge_t, nb_t], [nb_t])
            for c in range(4):
                for sd in range(2):
                    cv = conv[:, c, sd * 511:L_TOK:512]
                    cts = [conv_t[c][b] for b in range(4)]
                    stt(cv, nb[:, c, :, sd], cw(2 * sd, c), cv, ALU.mult, ALU.add, [nb_t, par_t] + cts, cts)
            if stop_after == ("post2", l):
                S.barrier()
                return True
            o_pool = bankpool([6, 7])
            for bi in ([0, 1, 2, 3] if last else [0, 1, 2, 3, 4]):
                t0, n, var = BLOCKS[bi]
                for m in range(KC):
                    bk, bt = o_pool.next()
                    for c in range(4):
                        mm(bk[:, :n], woc[:, c, m * P:(m + 1) * P], conv[:, c, t0:t0 + n], c == 0, c == 3, [slot_t["B"], conv_t[c][bi]], [bt])
                    stt(xT[:, m, t0:t0 + n], bk[:, :n], drv_ap(l, 1, 1, m, var), xT[:, m, t0:t0 + n], ALU.mult, ALU.add,
                        [bt, drv_t, x_t[m][bi]], [x_t[m][bi]])
            S.barrier()
            if stop_after == ("conv", l):
                return True
            memset(Vt[:, :, 64:65], 1.0, [Vones_t])

            s_pool = bankpool([0, 1, 2])
            oacc_pool = bankpool([3, 4])
            e_pool = bankpool([5])
            kv_pool = bankpool([0, 1, 2, 5])
            cnt_ = {"q": 0, "p": 0, "o": 0, "a": 0}
            qblocks = [0, 1, 2, 3] if last else [0, 1, 2, 3, 4]
            items = [(h, bi) for h in range(NH) for bi in qblocks]

            def expand_kv(h):
                for kb in range(17):
                    k0 = kb * 512
                    kn = min(512, NKEY - k0)
                    bk, bt = kv_pool.next()
                    if kb == 0:
                        src_t = [ckva_t[0], ckva_t[1]]
                    else:
                        src_t = [ckva_t[1 + (k0 - C_TOK) // L_TOK], ckva_t[1 + min(3, (k0 + kn - 1 - C_TOK) // L_TOK)]]
                    mm(bk[0:64, :kn], wuk[:, h * 64:(h + 1) * 64], ckv_all[:, k0:k0 + kn], True, True, [slot_t["C"], sA] + src_t, [bt])
                    evac_copy(Kt[0:64, k0:k0 + kn], bk[0:64, :kn], [bt], [K_t[kb]])
                for kg in range(9):
                    c0 = kg * 8
                    cn = min(8, NKC - c0)
                    bk, bt = kv_pool.next()
                    for i in range(cn):
                        kc = c0 + i
                        mm(bk[:, i * 64:(i + 1) * 64], ckv_all[:, kc * P:(kc + 1) * P], wuv[:, h * 64:(h + 1) * 64], True, True,
                           [slot_t["C"], sA] + ckva_t, [bt])
                    evac_copy(Vt[:, c0:c0 + cn, 0:64], bk[:, 0:cn * 64].rearrange("p (a b) -> p a b", b=64), [bt], [V_t[kg]])
                wo_h, wo_ht = woh[h % 2], woh_t[h % 2]
                dma("pool", [(wo_h[:, :], wout_d[l, h * 64:(h + 1) * 64, :])], [], [wo_ht], "d_woh%d" % (h % 2))

            def q_stages(h, bi):
                t0, n, var = BLOCKS[bi]
                qb_, qbt = qt[cnt_["q"] % 2], q_t[cnt_["q"] % 2]
                cnt_["q"] += 1

                def st_a():
                    qa = e_pool.next()
                    for k in range(2):
                        mm(qa[0][0:96, :n], wuq[:, k, h * 192:h * 192 + 96], cqn[:, k, t0:t0 + n], k == 0, k == 1, [slot_t["C"], cqn_t[k][bi]], [qa[1]])
                    act(qb_[0:64, :n], qa[0][0:64, :n], AF.Copy, [qa[1]], [qbt])
                    if var == 0:
                        tt(qr1[:, :n], qa[0][64:96, :n], tab[:, 0, t0:t0 + n], ALU.mult, [qa[1], tab_t], [qr1_t])
                    else:
                        act(qb_[64:96, :n], qa[0][64:96, :n], AF.Copy, [qa[1], qbt], [qbt])

                def st_b():
                    if var == 0:
                        qs = e_pool.next()
                        for k in range(2):
                            mm(qs[0][0:96, :n], wuq[:, k, h * 192 + 96:h * 192 + 192], cqn[:, k, t0:t0 + n], k == 0, k == 1, [slot_t["C"], cqn_t[k][bi]], [qs[1]])
                        tt(qr2[:, :n], qs[0][64:96, :n], tab[:, 1, t0:t0 + n], ALU.mult, [qs[1], tab_t], [qr2_t])
                        tt(qb_[64:96, :n], qr1[:, :n], qr2[:, :n], ALU.add, [qr1_t, qr2_t, qbt], [qbt])
                return [st_a, st_b], qb_, qbt

            def tail_stages(h, bi, ob, obt):
                t0, n, var = BLOCKS[bi]
                wo_h, wo_ht = woh[h % 2], woh_t[h % 2]
                os_, ost = osb[cnt_["o"] % 2], osb_t[cnt_["o"] % 2]
                cnt_["o"] += 1
                at_, att_ = att[cnt_["a"] % 2], att_t[cnt_["a"] % 2]
                cnt_["a"] += 1

                def st_a():
                    act(os_[:, :n], ob[0:65, :n], AF.Copy, [obt], [ost])
                    recip(os_[64:65, :n], os_[64:65, :n], [ost], [ost])

                def st_b():
                    bc = e_pool.next()
                    mm(bc[0][0:64, :n], ones_f[0:65, 0:64], os_[0:65, :n], True, True, [ost, const_t], [bc[1]])
                    tt(at_[:, :n], bc[0][0:64, :n], os_[0:64, :n], ALU.mult, [bc[1], ost], [att_])

                def st_c(m):
                    def f():
                        bk, bt = o_pool.next()
                        mm(bk[:, :n], wo_h[:, m * P:(m + 1) * P], at_[:, :n], True, True, [wo_ht, att_], [bt])
                        stt(xT[:, m, t0:t0 + n], bk[:, :n], drv_ap(l, 1, 1, m, var), xT[:, m, t0:t0 + n], ALU.mult, ALU.add,
                            [bt, drv_t, x_t[m][bi]], [x_t[m][bi]])
                    return f
                return [st_a, st_b] + [st_c(m) for m in range(KC)]

            pending = []
            TAIL_AT = {2, 5, 8, 10, 12, 14, 16, 18, 20, 22}

            def expand_kv(h):
                grp = 0
                for kb in range(17):
                    k0 = kb * 512
                    kn = min(512, NKEY - k0)
                    bk, bt = kv_pool.next()
                    if kb == 0:
                        src_t = [ckva_t[0], ckva_t[1]]
                    else:
                        src_t = [ckva_t[1 + (k0 - C_TOK) // L_TOK], ckva_t[1 + min(3, (k0 + kn - 1 - C_TOK) // L_TOK)]]
                    mm(bk[0:64, :kn], wuk[:, h * 64:(h + 1) * 64], ckv_all[:, k0:k0 + kn], True, True, [slot_t["C"], sA] + src_t, [bt])
                    evac_copy(Kt[0:64, k0:k0 + kn], bk[0:64, :kn], [bt], [K_t[kb]])
                    grp += 1
                    if grp % 2 == 0 and len(pending) > 10:
                        pending.pop(0)()
                for kg in range(9):
                    c0 = kg * 8
                    cn = min(8, NKC - c0)
                    bk, bt = kv_pool.next()
                    for i in range(cn):
                        kc = c0 + i
                        mm(bk[:, i * 64:(i + 1) * 64], ckv_all[:, kc * P:(kc + 1) * P], wuv[:, h * 64:(h + 1) * 64], True, True,
                           [slot_t["C"], sA] + ckva_t, [bt])
                    evac_copy(Vt[:, c0:c0 + cn, 0:64], bk[:, 0:cn * 64].rearrange("p (a b) -> p a b", b=64), [bt], [V_t[kg]])
                    grp += 1
                    if grp % 2 == 0 and len(pending) > 10:
                        pending.pop(0)()
                wo_h, wo_ht = woh[h % 2], woh_t[h % 2]
                dma("pool", [(wo_h[:, :], wout_d[l, h * 64:(h + 1) * 64, :])], [], [wo_ht], "d_woh%d" % (h % 2))

            next_q = None
            pend = []

            def pv(kc, pb, pbt, first, lastc, ob, obt, n):
                mm(ob[0:65, :n], Vt[:, kc, 0:65], pb[:, :n], first, lastc, [V_t[kc // 8], Vones_t, pbt], [obt])
            for it, (h, bi) in enumerate(items):
                t0, n, var = BLOCKS[bi]
                if bi == qblocks[0]:
                    while pend:
                        pv(*pend.pop(0))
                    expand_kv(h)
                while len(pending) > 10:
                    pending.pop(0)()
                if next_q is None:
                    qst, qb_, qbt = q_stages(h, bi)
                    for f in qst:
                        f()
                else:
                    qb_, qbt = next_q
                next_q = None
                nq_stages = []
                chunks = list(range(NKC)) if var == 0 else [0, 1]
                ob, obt = oacc_pool.next()
                for ci, kc in enumerate(chunks):
                    sb_, sbt = s_pool.next()
                    mm(sb_[:, :n], Kt[0:96, kc * P:(kc + 1) * P], qb_[0:96, :n], True, True, [K_t[kc // 4], Kpe_t[0 if kc < 2 else 1 + (kc - 2) // 16], qbt], [sbt])
                    pb, pbt = Pt[cnt_["p"] % 3], P_t[cnt_["p"] % 3]
                    cnt_["p"] += 1
                    act(pb[:, :n], sb_[:, :n], AF.Exp, [sbt], [pbt], scale=ATTN_SCALE)
                    pend.append((kc, pb, pbt, ci == 0, ci == len(chunks) - 1, ob, obt, n))
                    if len(pend) > 2:
                        pv(*pend.pop(0))
                    if ci in TAIL_AT and pending:
                        pending.pop(0)()
                    if ci == 30 and it + 1 < len(items):
                        nq_stages, nqb, nqt = q_stages(*items[it + 1])
                        next_q = (nqb, nqt)
                        nq_stages.pop(0)()
                    if ci == 36 and nq_stages:
                        nq_stages.pop(0)()
                for f in nq_stages:
                    f()
                pending.extend(tail_stages(h, bi, ob, obt))
            while pend:
                pv(*pend.pop(0))
            while pending:
                pending.pop(0)()
            S.barrier()
            return False

        def final(with_norm=True):
            AR.reset()
            fnw = AR.alloc([D], F32); fnw_t = T("fnw")
            ost = [AR.alloc([D], F32) for _ in range(2)]; ost_t = [T("ost0"), T("ost1")]
            junk = AR.alloc([512], F32); junk_t = T("junk")
            ssq = AR.alloc([16, 4], F32); ssq_t = [T("ssq%d" % i) for i in range(16)]
            dma("sp", [(fnw[:, :], fnw_d[:, :])], [], [fnw_t], "d_fnw")
            tp = bankpool([0, 1, 2, 3, 4, 5])
            for tk in range(L_TOK // P):
                bi = tk // 4
                hb = [tp.next(), tp.next()]
                for c in range(KC):
                    bk, bt = hb[c // 4]
                    tr(bk[:, (c % 4) * P:(c % 4 + 1) * P], xT[:, c, tk * P:(tk + 1) * P], [x_t[c][bi], const_t], [bt])
                sq_ = ssq[:, tk, :]
                sqt = ssq_t[tk]
                o_, ot = ost[tk % 2], ost_t[tk % 2]
                if with_norm:
                    for hf in range(2):
                        act(junk[:, :], hb[hf][0][:, :], AF.Square, [hb[hf][1], sqt], [junk_t, sqt], accum_out=sq_[:, hf:hf + 1])
                    tt(sq_[:, 2:3], sq_[:, 0:1], sq_[:, 1:2], ALU.add, [sqt], [sqt])
                    act(sq_[:, 3:4], sq_[:, 2:3], AF.Sqrt, [sqt, par_t], [sqt], bias=eps_ap, scale=1.0 / D)
                    recip(sq_[:, 2:3], sq_[:, 3:4], [sqt], [sqt])
                    for hf in range(2):
                        stt(o_[:, hf * 512:(hf + 1) * 512], hb[hf][0][:, :], sq_[:, 2:3], fnw[:, hf * 512:(hf + 1) * 512], ALU.mult, ALU.mult,
                            [hb[hf][1], sqt, fnw_t, ot], [ot])
                else:
                    for hf in range(2):
                        evac_copy(o_[:, hf * 512:(hf + 1) * 512], hb[hf][0][:, :], [hb[hf][1], ot], [ot])
                dma("sp", [(out_d[tk * P:(tk + 1) * P, :], o_[:, :])], [ot], [out_tiles[tk]], "d_out%d" % (tk % 2))

        S.barrier()
        stopped = False
        if stop_after != "prologue":
            for l in range(DEPTH):
                last = (l == DEPTH - 1)
                ffn(l, 0, [0, 1, 2, 3, 4])
                if stop_after == ("ffn1", l):
                    stopped = True
                    break
                if mixer(l):
                    stopped = True
                    break
                if stop_after == ("mixer", l):
                    stopped = True
                    break
                ffn(l, 1, [0, 1, 2, 3] if last else [0, 1, 2, 3, 4], first_slot="B")
                if stop_after == ("ffn2", l):
                    stopped = True
                    break
        else:
            stopped = True
        final(with_norm=not stopped)
        S.barrier(engines=("sp",))
        cnt = S.finalize()
        S.check_deadlock()
        sems = {}
        for ch in cnt:
            sems[ch] = es.enter_context(nc.semaphore("s_" + ch))
        block = es.enter_context(nc.Block())

        @block.tensor
        def _(eng):
            S.emit("pe", eng, sems)

        @block.scalar
        def _(eng):
            S.emit("act", eng, sems)

        @block.vector
        def _(eng):
            S.emit("dve", eng, sems)

        @block.sync
        def _(eng):
            S.emit("sp", eng, sems)

        @block.gpsimd
        def _(eng):
            S.emit("pool", eng, sems)
    return nc, AR.hi, cnt, {e: len(S.ops[e]) for e in ENGS}


def _fm(v):
    return np.ascontiguousarray(np.asarray(v, np.float32).reshape(-1, P).T)


def _rope_tables(q):
    t = np.arange(q * L_TOK, (q + 1) * L_TOK)
    row = (t // 64).astype(np.float32)
    col = (t % 64).astype(np.float32)
    d_axis = 16
    inv = (np.float32(10000.0) ** (-np.arange(0, d_axis, 2, dtype=np.float32) / np.float32(d_axis))).astype(np.float32)
    ar = row[:, None] * inv
    ac = col[:, None] * inv
    cr, sr, cc, sc = np.cos(ar), np.sin(ar), np.cos(ac), np.sin(ac)
    C = np.concatenate([cr, cr, cc, cc], axis=1).T
    Sg = np.concatenate([-sr, sr, -sc, sc], axis=1).T
    return np.ascontiguousarray(np.stack([C, Sg], axis=1).astype(np.float32)).astype(ml_dtypes.bfloat16)


_SW = np.concatenate([np.arange(8, 16), np.arange(0, 8), np.arange(24, 32), np.arange(16, 24)])


def _prep_shared(w_ada, b_ada, norm_w, w_ffn1_in, w_ffn1_out, w_ffn2_in, w_ffn2_out, w_in, q_norm_w, kv_norm_w,
                 w_uq, w_ukv, conv_w, w_out, final_norm_w):
    f = lambda a: np.ascontiguousarray(np.asarray(a, np.float32))
    w_in = f(w_in)
    cq, ckv, kpe = w_in[:, :, 0:256], w_in[:, :, 256:384], w_in[:, :, 384:416]
    gb, gc, xv = w_in[:, :, 416:928], w_in[:, :, 928:1440], w_in[:, :, 1440:1952]
    kpe_sw = kpe[:, :, _SW]
    w_in_r = np.concatenate([cq, ckv, gb, gc, xv, kpe, kpe_sw, kpe, kpe_sw], axis=2)
    w_uq = f(w_uq).reshape(DEPTH, 256, NH, 96)
    pe_sw = w_uq[:, :, :, 64:96][:, :, :, _SW]
    w_uq_r = np.concatenate([w_uq, w_uq[:, :, :, 0:64], pe_sw], axis=3).reshape(DEPTH, 256, NH * 192)
    w_ukv = f(w_ukv).reshape(DEPTH, 128, NH, 128)
    w_uk_r = np.ascontiguousarray(w_ukv[:, :, :, 0:64].reshape(DEPTH, 128, 512))
    w_uv_r = np.ascontiguousarray(w_ukv[:, :, :, 64:128].reshape(DEPTH, 128, 512))
    shared = {
        "w_ada": f(w_ada), "w_ffn1_in": f(w_ffn1_in), "w_ffn1_out": f(w_ffn1_out),
        "w_ffn2_in": f(w_ffn2_in), "w_ffn2_out": f(w_ffn2_out),
        "w_in_r": np.ascontiguousarray(w_in_r), "w_uq_r": np.ascontiguousarray(w_uq_r),
        "w_uk_r": w_uk_r, "w_uv_r": w_uv_r, "w_out": f(w_out),
        "fnw_bc": np.ascontiguousarray(np.broadcast_to(f(final_norm_w)[None, :], (P, D))),
        "ident": np.eye(P, dtype=np.float32),
    }
    par = np.zeros((P, NPAR), np.float32)
    for l in range(DEPTH):
        par[:, PC_BADA + l * 72:PC_BADA + (l + 1) * 72] = _fm(f(b_ada)[l])
        for s in range(3):
            par[:, PC_NW + (l * 3 + s) * 8:PC_NW + (l * 3 + s) * 8 + 8] = _fm(f(norm_w)[l, s])
        par[:, PC_QNW + l * 2:PC_QNW + l * 2 + 2] = _fm(f(q_norm_w)[l])
        par[:, PC_KVNW + l:PC_KVNW + l + 1] = _fm(f(kv_norm_w)[l])
        for k in range(3):
            par[:, PC_CONVW + (l * 3 + k) * 4:PC_CONVW + (l * 3 + k) * 4 + 4] = _fm(f(conv_w)[l, k])
    par[:, PC_EPS] = EPS
    return shared, par


_NC_CACHE = {}
_STOP_AFTER = None


def _in_maps(x, c, ctx, c_ctx, w_ada, b_ada, norm_w, w_ffn1_in, w_ffn1_out, w_ffn2_in, w_ffn2_out,
             w_in, q_norm_w, kv_norm_w, w_uq, w_ukv, conv_w, w_out, final_norm_w):
    x = np.asarray(x, np.float32)
    ctx = np.asarray(ctx, np.float32)
    c = np.asarray(c, np.float32)
    c_ctx = np.asarray(c_ctx, np.float32)
    shared, par0 = _prep_shared(w_ada, b_ada, norm_w, w_ffn1_in, w_ffn1_out, w_ffn2_in, w_ffn2_out, w_in,
                                q_norm_w, kv_norm_w, w_uq, w_ukv, conv_w, w_out, final_norm_w)
    in_maps = []
    for r in range(8):
        b, q = r // 4, r % 4
        par = par0.copy()
        cv = np.stack([_fm(c[b]), _fm(c_ctx)], axis=2)
        par[:, PC_CVEC:PC_CVEC + 16] = cv.reshape(P, 16)
        mask = np.zeros((4, 8), np.float32)
        for cc in range(4):
            if q - 1 >= 0:
                mask[q - 1, cc * 2 + 1] = 1.0
            if q + 1 < 4:
                mask[q + 1, cc * 2 + 0] = 1.0
        par[:, PC_MASK:PC_MASK + 32] = mask.reshape(1, 32)
        m = dict(shared)
        m["x"] = np.ascontiguousarray(x[b, q * L_TOK:(q + 1) * L_TOK, :])
        m["ctx"] = np.ascontiguousarray(ctx[b])
        m["par"] = par
        m["rope"] = _rope_tables(q)
        in_maps.append(m)
    return in_maps


def kernel(**inputs):
    in_maps = _in_maps(**inputs)
    if "nc" not in _NC_CACHE:
        _NC_CACHE["nc"] = build(_STOP_AFTER)[0]
    nc = _NC_CACHE["nc"]
    res = run_bass_kernel_spmd(nc, in_maps, core_ids=list(range(8)))
    out = np.empty((2, 8192, D), np.float32)
    for r in range(8):
        b, q = r // 4, r % 4
        out[b, q * L_TOK:(q + 1) * L_TOK, :] = res.results[r]["out"]
    return out
```
